# Optimizing a Trainium2 kernel written in Bass

```python
import math
import jax, jax.numpy as jnp
from jax import lax
import numpy as np

D_MODEL = 1024
BATCH = 8
SEQ = 2048
DEPTH = 4
DEC_BATCH = 128
DEC_SEQ = 1
PAST_LEN = 16384
PAGE_SIZE = 128

N_META = 16
N_MIXERS = 2
N_SSD_LAYERS = (DEPTH + 1) // 2
N_LRU_LAYERS = DEPTH // 2
CONV_WIDTH = 4
EPS = 1e-6
SSD_EXPAND = 2
D_INNER = SSD_EXPAND * D_MODEL
SSD_HEAD_DIM = 64
SSD_HEADS = D_INNER // SSD_HEAD_DIM
SSD_GROUPS = 8
SSD_HPG = SSD_HEADS // SSD_GROUPS
SSD_STATE = 128
SSD_CHUNK = 128
SSD_CONV_DIM = D_INNER + 2 * SSD_GROUPS * SSD_STATE
SSD_IN_DIM = D_INNER + SSD_CONV_DIM + SSD_HEADS
D_RNN = D_MODEL
LRU_BLOCKS = 8
LRU_BLOCK_W = D_RNN // LRU_BLOCKS
LRU_C = 8.0
D_FF = 4 * D_MODEL

kernel_name = "hybrid_ssd_rglru_decode_step"


def rms_norm(x, g):
    xf = x.astype(jnp.float32)
    y = xf * lax.rsqrt(jnp.mean(xf * xf, axis=-1, keepdims=True) + EPS)
    return (y * g.astype(jnp.float32)).astype(x.dtype)


def causal_conv(x, prev, w, b):
    L = x.shape[1]
    xp = jnp.concatenate([prev.astype(x.dtype), x], axis=1)
    out = b + sum(xp[:, k:k + L] * w[k] for k in range(CONV_WIDTH))
    return out, xp[:, -(CONV_WIDTH - 1):]


def segsum_exp(cs):
    Q = cs.shape[-1]
    diff = cs[..., :, None] - cs[..., None, :]
    mask = jnp.tril(jnp.ones((Q, Q), dtype=bool))
    return jnp.where(mask, jnp.exp(jnp.where(mask, diff, 0.0)), 0.0)


def ssd_chunked(xdt, a, Bm, Cm, h0, chunk):
    b, l = xdt.shape[:2]
    nc = l // chunk
    f32 = jnp.float32
    x = xdt.astype(f32).reshape(b, nc, chunk, SSD_GROUPS, SSD_HPG, SSD_HEAD_DIM)
    Bc = Bm.astype(f32).reshape(b, nc, chunk, SSD_GROUPS, SSD_STATE)
    Cc = Cm.astype(f32).reshape(b, nc, chunk, SSD_GROUPS, SSD_STATE)
    ac = a.astype(f32).reshape(b, nc, chunk, SSD_GROUPS, SSD_HPG)
    cs = jnp.cumsum(ac, axis=2)
    Lmat = segsum_exp(jnp.moveaxis(cs, 2, -1))
    cb = jnp.einsum("bclgn,bcsgn->bcgls", Cc, Bc)
    scores = cb[:, :, :, None] * Lmat
    y_diag = jnp.einsum("bcgrls,bcsgrp->bclgrp", scores, x)
    decay_to_end = jnp.exp(cs[:, :, -1:] - cs)
    chunk_states = jnp.einsum("bcsgn,bcsgrp->bcgrpn", Bc, x * decay_to_end[..., None])
    chunk_decay = jnp.exp(cs[:, :, -1])

    def step(h, inp):
        dec, st = inp
        return h * dec[..., None, None] + st, h

    h_init = h0.astype(f32).reshape(b, SSD_GROUPS, SSD_HPG, SSD_HEAD_DIM, SSD_STATE)
    h_final, h_prev = lax.scan(step, h_init, (jnp.moveaxis(chunk_decay, 1, 0), jnp.moveaxis(chunk_states, 1, 0)))
    h_prev = jnp.moveaxis(h_prev, 0, 1)
    y_off = jnp.einsum("bclgn,bcgrpn->bclgrp", Cc, h_prev) * jnp.exp(cs)[..., None]
    y = (y_diag + y_off).reshape(b, l, SSD_HEADS, SSD_HEAD_DIM)
    return y, h_final.reshape(b, SSD_HEADS, SSD_HEAD_DIM, SSD_STATE).astype(h0.dtype)


def ssd_mixer(u, conv_prev, h0, w_in, conv_w, conv_b, dt_bias, a_log, d_skip, norm_g, w_out, n_lead, chunk):
    b, l, _ = u.shape
    proj = u @ w_in
    z = proj[..., :D_INNER]
    xbc = proj[..., D_INNER:D_INNER + SSD_CONV_DIM]
    dt = proj[..., D_INNER + SSD_CONV_DIM:]
    xbc, conv_new = causal_conv(xbc, conv_prev, conv_w, conv_b)
    xbc = jax.nn.silu(xbc)
    xs = xbc[..., :D_INNER].reshape(b, l, SSD_HEADS, SSD_HEAD_DIM)
    Bm = xbc[..., D_INNER:D_INNER + SSD_GROUPS * SSD_STATE].reshape(b, l, SSD_GROUPS, SSD_STATE)
    Cm = xbc[..., D_INNER + SSD_GROUPS * SSD_STATE:].reshape(b, l, SSD_GROUPS, SSD_STATE)
    dt = jax.nn.softplus(dt.astype(jnp.float32) + dt_bias.astype(jnp.float32))
    a = dt * (-jnp.exp(a_log.astype(jnp.float32)))
    xdt = xs.astype(jnp.float32) * dt[..., None]
    if n_lead > 0:
        y1, h = ssd_chunked(xdt[:, :n_lead], a[:, :n_lead], Bm[:, :n_lead], Cm[:, :n_lead], h0, n_lead)
        y2, h = ssd_chunked(xdt[:, n_lead:], a[:, n_lead:], Bm[:, n_lead:], Cm[:, n_lead:], h, chunk)
        y = jnp.concatenate([y1, y2], axis=1)
    else:
        y, h = ssd_chunked(xdt, a, Bm, Cm, h0, chunk)
    y = y + xs.astype(jnp.float32) * d_skip.astype(jnp.float32)[:, None]
    y = y.reshape(b, l, D_INNER) * jax.nn.silu(z.astype(jnp.float32))
    yg = y.reshape(b, l, SSD_GROUPS, D_INNER // SSD_GROUPS)
    yg = yg * lax.rsqrt(jnp.mean(yg * yg, axis=-1, keepdims=True) + EPS)
    y = yg.reshape(b, l, D_INNER) * norm_g.astype(jnp.float32)
    return y.astype(u.dtype) @ w_out, conv_new, h


def rglru_mixer(u, conv_prev, h0, w_in, b_in, conv_w, conv_b, w_a, b_a, w_x, b_x, lam, w_out, b_out):
    b, l, _ = u.shape
    proj = u @ w_in + b_in
    gate = jax.nn.gelu(proj[..., :D_RNN], approximate=True)
    xr, conv_new = causal_conv(proj[..., D_RNN:], conv_prev, conv_w, conv_b)
    xb = xr.reshape(b, l, LRU_BLOCKS, LRU_BLOCK_W)
    r = jax.nn.sigmoid((jnp.einsum("blkc,kcd->blkd", xb, w_a).reshape(b, l, D_RNN) + b_a).astype(jnp.float32))
    i = jax.nn.sigmoid((jnp.einsum("blkc,kcd->blkd", xb, w_x).reshape(b, l, D_RNN) + b_x).astype(jnp.float32))
    log_a = -LRU_C * r * jax.nn.softplus(-lam.astype(jnp.float32))
    a = jnp.exp(log_a)
    mult = jnp.sqrt(-jnp.expm1(2.0 * log_a))
    bterm = mult * i * xr.astype(jnp.float32)
    bterm = bterm.at[:, 0].add(a[:, 0] * h0.astype(jnp.float32))

    def combine(c1, c2):
        a1, b1 = c1
        a2, b2 = c2
        return a1 * a2, a2 * b1 + b2

    _, h = lax.associative_scan(combine, (a, bterm), axis=1)
    y = (h * gate.astype(jnp.float32)).astype(u.dtype)
    return y @ w_out + b_out, conv_new, h[:, -1].astype(h0.dtype)


def sq_relu_mlp(x, w1, w2):
    h = jax.nn.relu(x @ w1)
    return (h * h) @ w2


def trunk(x, ssd_conv, ssd_h, lru_conv, lru_h, n_lead, chunk, p):
    n_ssd_conv, n_ssd_h, n_lru_conv, n_lru_h = [], [], [], []
    for i in range(DEPTH):
        h = rms_norm(x, p["norm_mix_pre"][i])
        j = i // N_MIXERS
        if i % N_MIXERS == 0:
            m, c_new, s_new = ssd_mixer(h, ssd_conv[j], ssd_h[j], p["ssd_w_in"][j], p["ssd_conv_w"][j],
                                        p["ssd_conv_b"][j], p["ssd_dt_bias"][j], p["ssd_a_log"][j],
                                        p["ssd_d"][j], p["ssd_norm"][j], p["ssd_w_out"][j], n_lead, chunk)
            n_ssd_conv.append(c_new)
            n_ssd_h.append(s_new)
        else:
            m, c_new, s_new = rglru_mixer(h, lru_conv[j], lru_h[j], p["lru_w_in"][j], p["lru_b_in"][j],
                                          p["lru_conv_w"][j], p["lru_conv_b"][j], p["lru_w_a"][j],
                                          p["lru_b_a"][j], p["lru_w_x"][j], p["lru_b_x"][j],
                                          p["lru_lambda"][j], p["lru_w_out"][j], p["lru_b_out"][j])
            n_lru_conv.append(c_new)
            n_lru_h.append(s_new)
        x = x + rms_norm(m, p["norm_mix_post"][i])
        h = rms_norm(x, p["norm_ffn_pre"][i])
        x = x + rms_norm(sq_relu_mlp(h, p["ffn_w1"][i], p["ffn_w2"][i]), p["norm_ffn_post"][i])
    return x, jnp.stack(n_ssd_conv), jnp.stack(n_ssd_h), jnp.stack(n_lru_conv), jnp.stack(n_lru_h)


def setup_inputs(seed: int = 0) -> dict:
    key = jax.random.key(seed)
    ks = jax.random.split(key, 40)
    nrm = jax.random.normal
    NA, NB = N_SSD_LAYERS, N_LRU_LAYERS
    dt0 = jnp.exp(jax.random.uniform(ks[10], (NA, SSD_HEADS), minval=math.log(1e-3), maxval=math.log(1e-1)))
    a_pow = jax.random.uniform(ks[20], (NB, D_RNN), minval=0.9, maxval=0.999)
    s = a_pow ** (1.0 / LRU_C)
    return {
        "x_prompt": nrm(ks[0], (BATCH, SEQ, D_MODEL), jnp.float32),
        "x_sample": nrm(ks[1], (DEC_BATCH, DEC_SEQ, D_MODEL), jnp.float32),
        "state_ssd_conv": nrm(ks[2], (NA, DEC_BATCH, CONV_WIDTH - 1, SSD_CONV_DIM), jnp.float32),
        "state_ssd_h": 0.1 * nrm(ks[3], (NA, DEC_BATCH, SSD_HEADS, SSD_HEAD_DIM, SSD_STATE), jnp.float32),
        "state_lru_conv": nrm(ks[4], (NB, DEC_BATCH, CONV_WIDTH - 1, D_RNN), jnp.float32),
        "state_lru_h": 0.5 * nrm(ks[5], (NB, DEC_BATCH, D_RNN), jnp.float32),
        "meta_tokens": nrm(ks[6], (N_META, D_MODEL), jnp.float32),
        "norm_mix_pre": 1.0 + 0.05 * nrm(ks[30], (DEPTH, D_MODEL), jnp.float32),
        "norm_mix_post": 1.0 + 0.05 * nrm(ks[31], (DEPTH, D_MODEL), jnp.float32),
        "norm_ffn_pre": 1.0 + 0.05 * nrm(ks[32], (DEPTH, D_MODEL), jnp.float32),
        "norm_ffn_post": 1.0 + 0.05 * nrm(ks[33], (DEPTH, D_MODEL), jnp.float32),
        "ssd_w_in": nrm(ks[7], (NA, D_MODEL, SSD_IN_DIM), jnp.float32) * D_MODEL ** -0.5,
        "ssd_conv_w": nrm(ks[8], (NA, CONV_WIDTH, SSD_CONV_DIM), jnp.float32) * CONV_WIDTH ** -0.5,
        "ssd_conv_b": 0.02 * nrm(ks[9], (NA, SSD_CONV_DIM), jnp.float32),
        "ssd_dt_bias": dt0 + jnp.log(-jnp.expm1(-dt0)),
        "ssd_a_log": jnp.log(jax.random.uniform(ks[11], (NA, SSD_HEADS), minval=1.0, maxval=16.0)),
        "ssd_d": 1.0 + 0.1 * nrm(ks[12], (NA, SSD_HEADS), jnp.float32),
        "ssd_norm": 1.0 + 0.05 * nrm(ks[13], (NA, D_INNER), jnp.float32),
        "ssd_w_out": nrm(ks[14], (NA, D_INNER, D_MODEL), jnp.float32) * D_INNER ** -0.5,
        "lru_w_in": nrm(ks[15], (NB, D_MODEL, 2 * D_RNN), jnp.float32) * D_MODEL ** -0.5,
        "lru_b_in": 0.02 * nrm(ks[16], (NB, 2 * D_RNN), jnp.float32),
        "lru_conv_w": nrm(ks[17], (NB, CONV_WIDTH, D_RNN), jnp.float32) * CONV_WIDTH ** -0.5,
        "lru_conv_b": 0.02 * nrm(ks[18], (NB, D_RNN), jnp.float32),
        "lru_w_a": nrm(ks[19], (NB, LRU_BLOCKS, LRU_BLOCK_W, LRU_BLOCK_W), jnp.float32) * LRU_BLOCK_W ** -0.5,
        "lru_b_a": 0.02 * nrm(ks[21], (NB, D_RNN), jnp.float32),
        "lru_w_x": nrm(ks[22], (NB, LRU_BLOCKS, LRU_BLOCK_W, LRU_BLOCK_W), jnp.float32) * LRU_BLOCK_W ** -0.5,
        "lru_b_x": 0.02 * nrm(ks[23], (NB, D_RNN), jnp.float32),
        "lru_lambda": jnp.log(s / (1.0 - s)),
        "lru_w_out": nrm(ks[24], (NB, D_RNN, D_MODEL), jnp.float32) * D_RNN ** -0.5,
        "lru_b_out": 0.02 * nrm(ks[25], (NB, D_MODEL), jnp.float32),
        "ffn_w1": nrm(ks[26], (DEPTH, D_MODEL, D_FF), jnp.float32) * D_MODEL ** -0.5,
        "ffn_w2": nrm(ks[27], (DEPTH, D_FF, D_MODEL), jnp.float32) * D_FF ** -0.5,
    }


def reference(x_prompt, x_sample, state_ssd_conv, state_ssd_h, state_lru_conv, state_lru_h, meta_tokens,
              norm_mix_pre, norm_mix_post, norm_ffn_pre, norm_ffn_post,
              ssd_w_in, ssd_conv_w, ssd_conv_b, ssd_dt_bias, ssd_a_log, ssd_d, ssd_norm, ssd_w_out,
              lru_w_in, lru_b_in, lru_conv_w, lru_conv_b, lru_w_a, lru_b_a, lru_w_x, lru_b_x, lru_lambda,
              lru_w_out, lru_b_out, ffn_w1, ffn_w2):
    p = dict(norm_mix_pre=norm_mix_pre, norm_mix_post=norm_mix_post, norm_ffn_pre=norm_ffn_pre,
             norm_ffn_post=norm_ffn_post, ssd_w_in=ssd_w_in, ssd_conv_w=ssd_conv_w, ssd_conv_b=ssd_conv_b,
             ssd_dt_bias=ssd_dt_bias, ssd_a_log=ssd_a_log, ssd_d=ssd_d, ssd_norm=ssd_norm, ssd_w_out=ssd_w_out,
             lru_w_in=lru_w_in, lru_b_in=lru_b_in, lru_conv_w=lru_conv_w, lru_conv_b=lru_conv_b,
             lru_w_a=lru_w_a, lru_b_a=lru_b_a, lru_w_x=lru_w_x, lru_b_x=lru_b_x, lru_lambda=lru_lambda,
             lru_w_out=lru_w_out, lru_b_out=lru_b_out, ffn_w1=ffn_w1, ffn_w2=ffn_w2)
    dt = x_prompt.dtype
    meta = jnp.broadcast_to(meta_tokens.astype(dt)[None], (BATCH, N_META, D_MODEL))
    xp = jnp.concatenate([meta, x_prompt], axis=1)
    z_ssd_conv = jnp.zeros((N_SSD_LAYERS, BATCH, CONV_WIDTH - 1, SSD_CONV_DIM), dt)
    z_ssd_h = jnp.zeros((N_SSD_LAYERS, BATCH, SSD_HEADS, SSD_HEAD_DIM, SSD_STATE), dt)
    z_lru_conv = jnp.zeros((N_LRU_LAYERS, BATCH, CONV_WIDTH - 1, D_RNN), dt)
    z_lru_h = jnp.zeros((N_LRU_LAYERS, BATCH, D_RNN), dt)
    yp, p_ssd_conv, p_ssd_h, p_lru_conv, p_lru_h = trunk(xp, z_ssd_conv, z_ssd_h, z_lru_conv, z_lru_h,
                                                         N_META, SSD_CHUNK, p)
    y_prompt = yp[:, N_META:]
    y_sample, s_ssd_conv, s_ssd_h, s_lru_conv, s_lru_h = trunk(x_sample, state_ssd_conv, state_ssd_h,
                                                               state_lru_conv, state_lru_h, 0, DEC_SEQ, p)
    return (y_prompt, y_sample, p_ssd_conv, p_ssd_h, p_lru_conv, p_lru_h,
            s_ssd_conv, s_ssd_h, s_lru_conv, s_lru_h)
```

```python
import numpy as np
from contextlib import ExitStack
import concourse.bass as bass
import concourse.mybir as mybir
from concourse.bass_utils import run_bass_kernel_spmd

F32 = mybir.dt.float32
BF16 = mybir.dt.bfloat16
AF = mybir.ActivationFunctionType
ALU = mybir.AluOpType
AX = mybir.AxisListType
ENGS = ["pe", "act", "dve", "pool", "sp"]

D = 1024
NPART = 4
TL = 528
SEQ = 512
NTAIL = 16
TTS = [(0, 264), (264, 264)]
EPS = 1e-6
NB = 16


class Sched:
    def __init__(self, nc):
        self.nc = nc
        self.ops = {e: [] for e in ENGS}
        self.count = {e: 0 for e in ENGS}
        self.sem = {e: nc.alloc_semaphore(name="s_" + e) for e in ENGS}
        self.seen = {e: {} for e in ENGS}
        self.last_w = {}
        self.readers = {}
        self.dsem = {}
        self.dcount = {}
        self.pending_pe = False

    def _semh(self, key):
        return self.sem[key] if key in self.sem else self.dsem[key]

    def _deps(self, eng, reads, writes):
        deps = []
        for b in reads:
            if b in self.last_w:
                deps.append(self.last_w[b])
        for b in writes:
            if b in self.last_w:
                deps.append(self.last_w[b])
            deps.extend(self.readers.get(b, ()))
        waits = {}
        for (k, v) in deps:
            if k == eng and eng == "pe":
                continue
            if self.seen[eng].get(k, 0) >= v:
                continue
            if waits.get(k, 0) < v:
                waits[k] = v
        for k, v in waits.items():
            self.seen[eng][k] = v
        return waits

    def _record(self, ev, reads, writes):
        for b in reads:
            self.readers.setdefault(b, []).append(ev)
        for b in writes:
            self.last_w[b] = ev
            self.readers[b] = []

    def op(self, eng, fn, reads=(), writes=(), inc=True):
        waits = self._deps(eng, reads, writes)
        ev = (eng, self.count[eng] + 1)
        if inc:
            self.count[eng] += 1
            if eng == "pe":
                self.pending_pe = False
        else:
            self.pending_pe = True
        self.ops[eng].append((waits, fn, eng if inc else None, 1))
        self._record(ev, reads, writes)

    def dma(self, queue, fn, key, reads=(), writes=()):
        if key not in self.dsem:
            self.dsem[key] = self.nc.alloc_semaphore(name="d_" + str(key))
            self.dcount[key] = 0
        waits = self._deps(queue, reads, writes)
        self.dcount[key] += 16
        ev = (key, self.dcount[key])
        self.ops[queue].append((waits, fn, key, 16))
        self._record(ev, reads, writes)

    def finish(self, queue="sp"):
        waits = dict(self.dcount)
        for e in ENGS:
            if e != queue and self.count[e] > 0:
                waits[e] = self.count[e]
        self.ops[queue].append((waits, None, None, 0))

    def emit(self):
        nc = self.nc
        assert not self.pending_pe
        engobj = {"pe": "tensor", "act": "scalar", "dve": "vector", "pool": "gpsimd", "sp": "sync"}
        with nc.Block() as block:
            for e in ENGS:
                ops = self.ops[e]

                def body(eo, ops=ops):
                    for (waits, fn, inc_key, amt) in ops:
                        for k, v in waits.items():
                            eo.wait_ge(self._semh(k), v)
                        if fn is None:
                            continue
                        ins = fn(eo)
                        if inc_key is not None:
                            ins.then_inc(self._semh(inc_key), amt)

                getattr(block, engobj[e])(body)


class VecMap:
    def __init__(self):
        self.cols = {}
        self.data = []
        self.n = 0

    def add(self, name, arr):
        arr = np.ascontiguousarray(arr, dtype=np.float32).reshape(128, -1)
        self.cols[name] = self.n
        self.data.append(arr)
        self.n += arr.shape[1]

    def table(self):
        return np.concatenate(self.data, axis=1)


def fm(v):
    return np.ascontiguousarray(np.asarray(v, dtype=np.float32).reshape(-1, 128).T)


def build_vec_layout():
    cols = {}
    n = 0

    def add(name, k):
        nonlocal n
        cols[name] = n
        n += k
    for i in range(4):
        for kind in ("mpre", "mpost", "fpre", "fpost"):
            add("%s%d" % (kind, i), 8)
    for j in range(2):
        add("scw%d" % j, 128)
        add("scb%d" % j, 32)
        add("snorm%d" % j, 16)
        add("sdt%d" % j, 1)
        add("lbin%d" % j, 16)
        add("lcw%d" % j, 32)
        add("lcb%d" % j, 8)
        add("lba%d" % j, 8)
        add("lbx%d" % j, 8)
        add("llam%d" % j, 8)
        add("lbout%d" % j, 8)
    return cols, n


VCOLS, NV = build_vec_layout()
CFG = dict(parts=(0, 1, 2, 3), layers=(0, 1, 2, 3), mixer=True, ffn=True, ncore=8, ssd_stop=99, groups=8, nchunks=99)


def build_nc():
    nc = bass.Bass("TRN2", target_bir_lowering=False)

    def din(name, shape):
        return nc.dram_tensor(name, list(shape), F32, kind="ExternalInput").ap()

    def dout(name, shape):
        return nc.dram_tensor(name, list(shape), F32, kind="ExternalOutput").ap()

    xin = din("xin", [128, 8, NPART, TL])
    cst = din("cst", [128, 6, 128])
    vec = din("vec", [128, NV])
    vecb = din("vecb", [1, 2 * 64])
    w_ssd_in = din("w_ssd_in", [2, 8, 2, 128, 8, 384])
    w_ssd_dt = din("w_ssd_dt", [2, 128, 8, 32])
    w_ssd_out = din("w_ssd_out", [2, 4, 128, 16, 256])
    w_lru_in = din("w_lru_in", [2, 8, 128, 8, 256])
    w_lru_ax = din("w_lru_ax", [2, 128, 8, 256])
    w_lru_out = din("w_lru_out", [2, 2, 128, 8, 512])
    w_ffn1 = din("w_ffn1", [4, 8, 128, 8, 512])
    w_ffn2 = din("w_ffn2", [4, 8, 128, 32, 128])
    s_sconv = din("s_sconv", [2, 128, 32, NB, 3])
    s_sh = din("s_sh", [2, NB, 128, 2048])
    s_lconv = din("s_lconv", [2, 128, 8, NB, 3])
    s_lh = din("s_lh", [2, 128, 8, NB])

    yout = dout("yout", [128, 8, NPART, TL])
    o_psconv = dout("o_psconv", [2, 128, 32, 3])
    o_ssconv = dout("o_ssconv", [2, 128, 32, NB, 3])
    o_psh = dout("o_psh", [2, 128, 2048])
    o_ssh = dout("o_ssh", [2, NB, 128, 2048])
    o_plconv = dout("o_plconv", [2, 128, 8, 3])
    o_slconv = dout("o_slconv", [2, 128, 8, NB, 3])
    o_plh = dout("o_plh", [2, 8, 128, 1])
    o_slh = dout("o_slh", [2, 128, 8, NB])

    es = ExitStack()
    with es:
        def sb(name, shape, dt=F32):
            return es.enter_context(nc.sbuf_tensor(name, list(shape), dt))

        S = Sched(nc)
        X = sb("X", [128, 8, TL])
        M = sb("M", [128, 8, TL])
        HB = sb("HB", [128, 8, TL], BF16)
        HT = sb("HT", [128, 2, 2048])
        HTB = sb("HTB", [128, 2048], BF16)
        NRING = 4
        RING = [sb("ring%d" % i, [128, 4096], BF16) for i in range(NRING)]
        CST = sb("CST", [128, 6, 128])
        IDB = sb("IDB", [128, 128], BF16)
        ONEB = sb("ONEB", [128, 128], BF16)
        VEC = sb("VEC", [128, NV])
        ALOGB = sb("ALOGB", [128, 128])
        ANEG = sb("ANEG", [128, 2, 32])
        LSC = sb("LSC", [128, 2, 8])
        SQ = sb("SQ", [128, 8, 264], BF16)
        RS = sb("RS", [128, 2, 264])
        WDT = sb("WDT", [128, 8, 32], BF16)
        WAX = sb("WAX", [128, 8, 256], BF16)
        SCARRY = sb("SCARRY", [128, 2, 32, 3])
        LCARRY = sb("LCARRY", [128, 2, 8, 3])
        HCARRY = sb("HCARRY", [128, 2, 8])
        ARENA_F32 = 22300
        ARENA = sb("ARENA", [128, ARENA_F32])
        PS = [es.enter_context(nc.psum_tensor("ps%d" % i, [128, 512], F32)) for i in range(6)]
        PB = es.enter_context(nc.psum_tensor("psb", [128, 1024], BF16))
        PB2 = es.enter_context(nc.psum_tensor("psb2", [128, 1024], BF16))

        IDF = CST[:, 0, :]
        VM = CST[:, 1, :]
        MLOW = CST[:, 2, :]
        ONEF = CST[:, 3, :]
        I16B = CST[:, 4:6, :]

        st = {"bank": 0, "ring": 0, "uid": 0}

        def pbank():
            b = st["bank"]
            st["bank"] = (b + 1) % CFG.get("nbanks", 6)
            return b

        def vcol(name, off=0, n=1):
            c = VCOLS[name] + off
            return VEC[:, c:c + n]

        def act(out, in_, func, reads, writes, **kw):
            S.op("act", lambda e: e.activation(out=out, in_=in_, func=func, **kw), reads, writes)

        def tt(out, in0, in1, op, reads, writes, eng="dve"):
            S.op(eng, lambda e: e.tensor_tensor(out=out, in0=in0, in1=in1, op=op), reads, writes)

        def ts(out, in0, s1, s2, op0, op1, reads, writes, eng="dve"):
            if op1 is None:
                S.op(eng, lambda e: e.tensor_scalar(out=out, in0=in0, scalar1=s1, scalar2=None, op0=op0), reads, writes)
            else:
                S.op(eng, lambda e: e.tensor_scalar(out=out, in0=in0, scalar1=s1, scalar2=s2, op0=op0, op1=op1), reads, writes)

        def stt(out, in0, scalar, in1, op0, op1, reads, writes):
            S.op("dve", lambda e: e.scalar_tensor_tensor(out=out, in0=in0, scalar=scalar, in1=in1, op0=op0, op1=op1), reads, writes)

        def mm(out, lhsT, rhs, start, stop, reads, writes, inc):
            S.op("pe", lambda e: e.matmul(out, lhsT=lhsT, rhs=rhs, start=start, stop=stop), reads, writes, inc=inc)

        def tr(out, in_, ident, reads, writes, inc=True):
            S.op("pe", lambda e: e.transpose(out, in_, ident), reads, writes, inc=inc)

        def cp(out, in_, reads, writes, eng="dve"):
            if eng == "act":
                act(out, in_, AF.Copy, reads, writes)
            else:
                S.op(eng, lambda e: e.tensor_copy(out=out, in_=in_), reads, writes)

        def wload(src_ap, n1, n2):
            i = st["ring"]
            st["ring"] = (i + 1) % NRING
            view = RING[i][:, 0:n1 * n2].rearrange("p (a b) -> p a b", a=n1)
            key = "ring%d" % i
            S.dma("pool", lambda e: e.dma_start(out=view, in_=src_ap), key, writes=[key])
            return view, key

        S.dma("sp", lambda e: e.dma_start(out=CST[:], in_=cst), "CST", writes=["CST"])
        S.dma("sp", lambda e: e.dma_start(out=VEC[:], in_=vec), "VEC", writes=["VEC"])
        S.dma("sp", lambda e: e.dma_start(out=ALOGB[:], in_=vecb.partition_broadcast(128)), "ALOGB", writes=["ALOGB"])
        cp(IDB[:], IDF, ["CST"], ["IDB"])
        cp(ONEB[:], ONEF, ["CST"], ["ONEB"])
        for j in range(2):
            act(ANEG[:, j, :], ALOGB[:, j * 64:j * 64 + 32], AF.Exp, ["ALOGB"], ["ANEG"])
        ts(ANEG[:], ANEG[:], -1.0, None, ALU.mult, None, ["ANEG"], ["ANEG"])
        for j in range(2):
            act(LSC[:, j, :], vcol("llam%d" % j, 0, 8), AF.Exp, ["VEC"], ["LSC"], scale=-1.0)
        act(LSC[:], LSC[:], AF.Ln, ["LSC"], ["LSC"], bias=1.0)
        ts(LSC[:], LSC[:], -8.0, None, ALU.mult, None, ["LSC"], ["LSC"])
        VECH = sb("VECH", [128, NV])
        ts(VECH[:], VEC[:], 0.5, None, ALU.mult, None, ["VEC"], ["VECH"])
        NHALF = sb("NHALF", [128, 2])
        S.op("dve", lambda e: e.memset(NHALF[:], -0.5), [], ["NHALF"])
        S.op("dve", lambda e: e.memset(SCARRY[:], 0.0), [], ["SCARRY%d_%d" % (a_, b_) for a_ in range(2) for b_ in range(32)])
        S.op("dve", lambda e: e.memset(LCARRY[:], 0.0), [], ["LCARRY%d_%d" % (a_, b_) for a_ in range(2) for b_ in range(8)])
        S.op("dve", lambda e: e.memset(HCARRY[:], 0.0), [], ["HCARRY%d_%d" % (a_, b_) for a_ in range(2) for b_ in range(8)])
        S.op("dve", lambda e: e.memset(HT[:], 0.0), [], ["HT%d_%d" % (jj, gg) for jj in range(2) for gg in range(8)])

        def dskb(j, g):
            return ALOGB[:, j * 64 + 32 + 4 * g: j * 64 + 32 + 4 * g + 4]

        def rstd_for(src, srckeys, ti):
            off, n = TTS[ti]
            b = pbank()
            pk = "P%d" % b
            for kc in range(8):
                sq = SQ[:, kc, 0:n]
                sk = "SQ%d" % kc
                act(sq, src[:, kc, off:off + n], AF.Square, srckeys, [sk])
                mm(PS[b][:, 0:n], ONEB[:], sq, kc == 0, kc == 7, ["ONEB", sk], [pk], inc=(kc == 7))
            rs = RS[:, ti % 2, 0:n]
            rk = "RS%d" % (ti % 2)
            act(rs, PS[b][:, 0:n], AF.Sqrt, [pk, "EPS"], [rk], scale=1.0 / D, bias=EPS_AP[:, 0:1])
            S.op("dve", lambda e: e.reciprocal(out=rs, in_=rs), [rk], [rk])
            return rs, rk

        EPS_T = sb("EPS_T", [128, 2])
        EPS_AP = EPS_T
        S.op("dve", lambda e: e.memset(EPS_T[:], EPS), [], ["EPS"])

        def prenorm(gname, srcdep):
            for ti, (off, n) in enumerate(TTS):
                rs, rk = rstd_for(X, ["X%d" % ti] , ti)
                for kc in range(8):
                    stt(HB[:, kc, off:off + n], X[:, kc, off:off + n], vcol(gname, kc), rs, ALU.mult, ALU.mult,
                        ["X%d" % ti, rk, "VEC", "EPS"], ["HB%d" % ti])

        def postnorm(gname):
            for ti, (off, n) in enumerate(TTS):
                rs, rk = rstd_for(M, ["M0", "M1"], ti)
                for kc in range(8):
                    stt(M[:, kc, off:off + n], M[:, kc, off:off + n], vcol(gname, kc), rs, ALU.mult, ALU.mult,
                        ["M%d" % ti, rk, "VEC", "EPS"], ["M%d" % ti])
                    tt(X[:, kc, off:off + n], X[:, kc, off:off + n], M[:, kc, off:off + n], ALU.add,
                       ["M%d" % ti, "X%d" % ti], ["X%d" % ti])

        HBK = ["HB0", "HB1"]

        def ffn(layer):
            HID = ARENA[:, 0:8448].bitcast(BF16).rearrange("p (a b) -> p a b", a=32)
            R = ARENA[:, 8448:8448 + 528].rearrange("p (a b) -> p a b", a=2)
            for q in range(8):
                W, wk = wload(w_ffn1[layer, q], 8, 512)
                for jj in range(4):
                    j = 4 * q + jj
                    for ti, (off, n) in enumerate(TTS):
                        b = pbank()
                        pk = "P%d" % b
                        for kc in range(8):
                            mm(PS[b][:, 0:n], W[:, kc, jj * 128:(jj + 1) * 128], HB[:, kc, off:off + n], kc == 0, kc == 7,
                               [wk, "HB%d" % ti], [pk], inc=(kc == 7))
                        r = R[:, ti, 0:n]
                        act(r, PS[b][:, 0:n], AF.Relu, [pk], ["R%d" % ti])
                        tt(HID[:, j, off:off + n], r, r, ALU.mult, ["R%d" % ti], ["HID%d_%d" % (j, ti)])
            for c in range(8):
                W, wk = wload(w_ffn2[layer, c], 32, 128)
                for ti, (off, n) in enumerate(TTS):
                    b = pbank()
                    pk = "P%d" % b
                    for j in range(32):
                        mm(PS[b][:, 0:n], W[:, j, :], HID[:, j, off:off + n], j == 0, j == 31,
                           [wk, "HID%d_%d" % (j, ti)], [pk], inc=(j == 31))
                    cp(M[:, c, off:off + n], PS[b][:, 0:n], [pk], ["M%d" % ti], eng="act")

        def run_streams(factories, nslots, extra=None, stag=0):
            pending = list(factories)
            free = list(range(nslots))
            active = []
            extra = list(extra or [])
            rounds = 0
            last_start = -10 ** 9
            while pending or active or extra:
                rounds += 1
                while pending and free and (not active or rounds - last_start >= stag):
                    s_ = free.pop(0)
                    active.append((pending.pop(0)(s_), s_))
                    last_start = rounds
                for item in list(active):
                    gen, s_ = item
                    try:
                        next(gen)
                    except StopIteration:
                        active.remove(item)
                        free.append(s_)
                for gen in list(extra):
                    try:
                        next(gen)
                    except StopIteration:
                        extra.remove(gen)

        def run_all(gen):
            for _ in gen:
                pass

        def conv_block(pc, pck, outf, outk, wname, woff, bname, boff, W, sample_prev, prevk, tab=None):
            tabv = VEC if tab is None else tab
            tabk = "VEC" if tab is None else "VECH"
            w = [tabv[:, VCOLS[wname] + woff + k:VCOLS[wname] + woff + k + 1] for k in range(4)]
            bb = tabv[:, VCOLS[bname] + boff:VCOLS[bname] + boff + 1]
            act(outf[:, 0:W], pc[:, 0:W], AF.Identity, [pck, tabk], [outk], scale=w[0], bias=bb)
            yield
            for k in (1, 2, 3):
                stt(outf[:, 0:W], pc[:, k:k + W], w[k], outf[:, 0:W], ALU.mult, ALU.add, [pck, tabk, outk], [outk])
                yield
            if sample_prev is not None:
                o = outf[:, SEQ:TL]
                act(o, pc[:, 3 + SEQ:3 + TL], AF.Identity, [pck, tabk], [outk], scale=w[3], bias=bb)
                for k in range(3):
                    stt(o, sample_prev[:, :, k], w[k], o, ALU.mult, ALU.add, [prevk, tabk, outk], [outk])
                yield

        def lru(j, part):
            kind = ["seq", "none", "none", "sample"][part]
            W_ = TL if kind == "seq" else SEQ
            Y = ARENA[:, 0:2112].bitcast(BF16).rearrange("p (a b) -> p a b", a=8)
            o = 2112

            def carve(nf32):
                nonlocal o
                v = ARENA[:, o:o + nf32]
                o += nf32
                return v
            NSL = CFG.get("lru_slots", 3)
            T = []
            for s_ in range(NSL):
                d = {}
                d["PC"] = carve(532)[:, 0:531]
                for nm in ("G", "XR", "Rg", "Ig", "Aa", "Tm", "H"):
                    d[nm] = carve(528)
                d["XRB"] = carve(264).bitcast(BF16)
                d["PREV"] = carve(48).rearrange("p (a b) -> p a b", a=NB)
                d["H0S"] = carve(16)
                d["OLC"] = carve(48).rearrange("p (a b) -> p a b", a=NB)
                T.append(d)
            assert o <= ARENA_F32, o
            S.dma("pool", lambda e: e.dma_start(out=WAX[:], in_=w_lru_ax[j]), "WAX", reads=HBK, writes=["WAX"])

            def block(k, s_):
                d = T[s_]
                PC, G, XR, XRB, Rg, Ig, Aa, Tm, H = d["PC"], d["G"], d["XR"], d["XRB"], d["Rg"], d["Ig"], d["Aa"], d["Tm"], d["H"]
                PREV, H0S, OLC = d["PREV"], d["H0S"], d["OLC"]
                K = lambda n: "%s_%d" % (n, s_)
                Wt, wk = wload(w_lru_in[j, k], 8, 256)
                cp(PC[:, 0:3], LCARRY[:, j, k, :], ["LCARRY%d_%d" % (j, k)] + HBK, [K("PC")])
                if kind == "sample":
                    S.dma("sp", lambda e: e.dma_start(out=PREV, in_=s_lconv[j, :, k]), K("PREVL"), reads=HBK, writes=[K("PREVL")])
                    S.dma("sp", lambda e: e.dma_start(out=H0S, in_=s_lh[j, :, k]), K("H0S"), reads=HBK, writes=[K("H0S")])
                yield
                for ti, (off, n) in enumerate(TTS):
                    b = pbank()
                    pk = "P%d" % b
                    for kc in range(8):
                        mm(PS[b][:, 0:n], Wt[:, kc, 0:128], HB[:, kc, off:off + n], kc == 0, kc == 7, [wk, "HB%d" % ti], [pk], inc=(kc == 7))
                    act(G[:, off:off + n], PS[b][:, 0:n], AF.Identity, [pk, "VEC"], [K("G")], bias=vcol("lbin%d" % j, k))
                    yield
                    b = pbank()
                    pk = "P%d" % b
                    for kc in range(8):
                        mm(PS[b][:, 0:n], Wt[:, kc, 128:256], HB[:, kc, off:off + n], kc == 0, kc == 7, [wk, "HB%d" % ti], [pk], inc=(kc == 7))
                    act(PC[:, 3 + off:3 + off + n], PS[b][:, 0:n], AF.Identity, [pk, "VEC"], [K("PC")], bias=vcol("lbin%d" % j, 8 + k))
                    yield
                tt(Tm, G, G, ALU.mult, [K("G")], [K("Tm")])
                yield
                ts(Tm, Tm, 0.044715, 1.0, ALU.mult, ALU.add, [K("Tm")], [K("Tm")])
                yield
                tt(Tm, Tm, G, ALU.mult, [K("Tm"), K("G")], [K("Tm")])
                yield
                act(Tm, Tm, AF.Sigmoid, [K("Tm")], [K("Tm")], scale=1.5957691216)
                yield
                tt(G, G, Tm, ALU.mult, [K("G"), K("Tm")], [K("G")])
                yield
                yield from conv_block(PC, K("PC"), XR, K("XR"), "lcw%d" % j, 4 * k, "lcb%d" % j, k, W_,
                                      PREV if kind == "sample" else None, K("PREVL"))
                cp(LCARRY[:, j, k, :], PC[:, W_:W_ + 3], [K("PC")], ["LCARRY%d_%d" % (j, k)])
                if kind == "sample":
                    cp(OLC[:, :, 0:2], PREV[:, :, 1:3], [K("PREVL")], [K("OLC")])
                    cp(OLC[:, :, 2], PC[:, 3 + SEQ:3 + TL], [K("PC")], [K("OLC")])
                    S.dma("sp", lambda e: e.dma_start(out=o_slconv[j, :, k], in_=OLC), K("OLCD"), reads=[K("OLC")])
                    S.dma("sp", lambda e: e.dma_start(out=o_plconv[j, :, k], in_=PC[:, SEQ:SEQ + 3]), K("OPLC"), reads=[K("PC")])
                cp(XRB, XR, [K("XR")], [K("XRB")], eng="act")
                yield
                for ti, (off, n) in enumerate(TTS):
                    b = pbank()
                    pk = "P%d" % b
                    mm(PS[b][:, 0:n], WAX[:, k, 0:128], XRB[:, off:off + n], True, True, ["WAX", K("XRB")], [pk], inc=True)
                    act(Rg[:, off:off + n], PS[b][:, 0:n], AF.Sigmoid, [pk, "VEC"], [K("Rg")], bias=vcol("lba%d" % j, k))
                    yield
                    b = pbank()
                    pk = "P%d" % b
                    mm(PS[b][:, 0:n], WAX[:, k, 128:256], XRB[:, off:off + n], True, True, ["WAX", K("XRB")], [pk], inc=True)
                    act(Ig[:, off:off + n], PS[b][:, 0:n], AF.Sigmoid, [pk, "VEC"], [K("Ig")], bias=vcol("lbx%d" % j, k))
                    yield
                act(Aa, Rg, AF.Exp, [K("Rg"), "LSC"], [K("Aa")], scale=LSC[:, j, k:k + 1])
                yield
                tt(Tm, Aa, Aa, ALU.mult, [K("Aa")], [K("Tm")])
                yield
                ts(Tm, Tm, -1.0, 1.0, ALU.mult, ALU.add, [K("Tm")], [K("Tm")])
                yield
                act(Tm, Tm, AF.Sqrt, [K("Tm")], [K("Tm")])
                yield
                tt(Tm, Tm, Ig, ALU.mult, [K("Tm"), K("Ig")], [K("Tm")])
                yield
                tt(Tm, Tm, XR, ALU.mult, [K("Tm"), K("XR")], [K("Tm")])
                yield
                S.op("dve", lambda e: e.tensor_tensor_scan(out=H[:, 0:W_], data0=Aa[:, 0:W_], data1=Tm[:, 0:W_],
                                                           initial=HCARRY[:, j, k:k + 1], op0=ALU.mult, op1=ALU.add),
                     [K("Aa"), K("Tm"), "HCARRY%d_%d" % (j, k)], [K("H")])
                yield
                cp(HCARRY[:, j, k:k + 1], H[:, W_ - 1:W_], [K("H")], ["HCARRY%d_%d" % (j, k)])
                if kind == "sample":
                    tt(H[:, SEQ:TL], Aa[:, SEQ:TL], H0S, ALU.mult, [K("Aa"), K("H0S")], [K("H")])
                    tt(H[:, SEQ:TL], H[:, SEQ:TL], Tm[:, SEQ:TL], ALU.add, [K("H"), K("Tm")], [K("H")])
                    S.dma("sp", lambda e: e.dma_start(out=o_slh[j, :, k], in_=H[:, SEQ:TL]), K("OSLH"), reads=[K("H")])
                    S.dma("sp", lambda e: e.dma_start(out=o_plh[j, k], in_=H[:, SEQ - 1:SEQ]), K("OPLH"), reads=[K("H")])
                elif kind == "none":
                    S.op("dve", lambda e: e.memset(H[:, SEQ:TL], 0.0), [K("H")], [K("H")])
                yield
                tt(Y[:, k, :], H, G, ALU.mult, [K("H"), K("G")], ["Y%d" % k])
                yield

            run_streams([(lambda s_, k=k: block(k, s_)) for k in range(8)], NSL, stag=CFG.get("lru_stag", 0))
            for q in range(2):
                Wt, wk = wload(w_lru_out[j, q], 8, 512)
                for cc in range(4):
                    c = 4 * q + cc
                    for ti, (off, n) in enumerate(TTS):
                        b = pbank()
                        pk = "P%d" % b
                        for kc in range(8):
                            mm(PS[b][:, 0:n], Wt[:, kc, cc * 128:(cc + 1) * 128], Y[:, kc, off:off + n], kc == 0, kc == 7,
                               [wk, "Y%d" % kc], [pk], inc=(kc == 7))
                        act(M[:, c, off:off + n], PS[b][:, 0:n], AF.Identity, [pk, "VEC"], ["M%d" % ti], bias=vcol("lbout%d" % j, c))

        def ssd(j, part):
            kind = ["seq", "none", "none", "sample"][part]
            W_ = TL if kind == "seq" else SEQ
            nch = 5 if kind == "seq" else 4
            o = 0

            def carve(nf32):
                nonlocal o
                v = ARENA[:, o:o + nf32]
                o += nf32
                return v
            YF = carve(4224).bitcast(BF16).rearrange("p (a b) -> p a b", a=16)
            XBC2 = [carve(1056).bitcast(BF16).rearrange("p (a b) -> p a b", a=4) for _ in range(3)]
            ZS2 = [carve(528).bitcast(BF16).rearrange("p (a b) -> p a b", a=2) for _ in range(3)]
            PC = carve(532)[:, 0:531]
            CV = carve(528)
            TZ = carve(528)
            DT = carve(528)
            DTM = carve(160).rearrange("p (a b) -> p a b", a=5)
            ATM = carve(160).rearrange("p (a b) -> p a b", a=5)
            ECS = carve(160).rearrange("p (a b) -> p a b", a=5)
            DTE = carve(160).rearrange("p (a b) -> p a b", a=5)
            CD = carve(160).rearrange("p (a b) -> p a b", a=5)
            AMASK = carve(512).rearrange("p (a b) -> p a b", a=NB)
            DECB = carve(512).rearrange("p (a b) -> p a b", a=NB)
            NSL = 2
            TS_ = []
            for s_ in range(NSL):
                d = {}
                for nm in ("XTM", "XDT", "XDTE", "ZTM", "Y3"):
                    d[nm] = carve(128).bitcast(BF16)
                d["BTM"] = carve(64).bitcast(BF16)
                d["CBM"] = carve(64).bitcast(BF16)
                d["AL"] = carve(512).rearrange("p (a b) -> p a b", a=4)
                d["ED"] = carve(256).bitcast(BF16).rearrange("p (a b) -> p a b", a=4)
                d["SC"] = carve(256).bitcast(BF16).rearrange("p (a b) -> p a b", a=4)
                for nm in ("Y1", "Y2", "JK"):
                    d[nm] = carve(256)
                d["SS"] = carve(2)
                TS_.append(d)
            CMASK = carve(256).rearrange("p (a b) -> p a b", a=NB)
            BMASK = carve(1024).bitcast(BF16).rearrange("p (a b) -> p a b", a=NB)
            SIN = carve(512).rearrange("p (a b) -> p a b", a=2)
            SNEW = carve(512).rearrange("p (a b) -> p a b", a=2)
            PREV = carve(48).rearrange("p (a b) -> p a b", a=NB)
            OSC = carve(48).rearrange("p (a b) -> p a b", a=NB)
            assert o <= ARENA_F32, o
            hk = "HT%d" % j
            hkg = lambda g: "HT%d_%d" % (j, g)
            hbg = lambda g: "HTB_%d" % g
            allhk = [hkg(g) for g in range(8)]
            aneg = ANEG[:, j, :]

            S.dma("pool", lambda e: e.dma_start(out=WDT[:], in_=w_ssd_dt[j]), "WDT", reads=HBK, writes=["WDT"])
            cp(HTB[:], HT[:, j, :], allhk + HBK, [hbg(g) for g in range(8)], eng="act")
            for ti, (off, n) in enumerate(TTS):
                b = pbank()
                pk = "P%d" % b
                for kc in range(8):
                    mm(PS[b][0:32, 0:n], WDT[:, kc, :], HB[:, kc, off:off + n], kc == 0, kc == 7, ["WDT", "HB%d" % ti], [pk], inc=(kc == 7))
                act(DT[0:32, off:off + n], PS[b][0:32, 0:n], AF.Exp, [pk, "VEC"], ["DT"], bias=VEC[0:32, VCOLS["sdt%d" % j]:VCOLS["sdt%d" % j] + 1])
            act(DT[0:32, :], DT[0:32, :], AF.Ln, ["DT"], ["DT"], bias=1.0)
            chunks = [(c * 128, 128) for c in range(4)] + [(SEQ, NTAIL)]
            ncht = 5 if kind != "none" else 4
            for ch in range(ncht):
                c0, Q = chunks[ch]
                b = pbank()
                pk = "P%d" % b
                tr(PS[b][0:Q, 0:32], DT[0:32, c0:c0 + Q], IDF[0:32, 0:32], ["DT", "CST"], [pk])
                cp(DTM[0:Q, ch, :], PS[b][0:Q, 0:32], [pk], ["DTM"])
                tt(ATM[0:Q, ch, :], DTM[0:Q, ch, :], aneg[0:Q, :], ALU.mult, ["DTM", "ANEG"], ["ATM"])
            for ch in range(nch):
                c0, Q = chunks[ch]
                for (dst, lhs, dk) in ((ECS, VM, "ECS"), (DTE, MLOW, "DTE")):
                    b = pbank()
                    pk = "P%d" % b
                    mm(PS[b][0:Q, 0:32], lhs[0:Q, 0:Q], ATM[0:Q, ch, :], True, True, ["CST", "ATM"], [pk], inc=True)
                    act(dst[0:Q, ch, :], PS[b][0:Q, 0:32], AF.Exp, [pk], [dk])
                b = pbank()
                pk = "P%d" % b
                mm(PS[b][:, 0:32], ONEF[0:Q, :], ATM[0:Q, ch, :], True, True, ["CST", "ATM"], [pk], inc=True)
                act(CD[:, ch, :], PS[b][:, 0:32], AF.Exp, [pk], ["CD"])
            if kind == "sample":
                tt(AMASK[0:NB], ATM[0:NB, 4, :].unsqueeze(1).to_broadcast([NB, NB, 32]),
                   IDF[0:NB, 0:NB].unsqueeze(2).to_broadcast([NB, NB, 32]), ALU.mult, ["ATM", "CST"], ["AMASK"])
                b = pbank()
                pk = "P%d" % b
                mm(PS[b][:, 0:512], ONEF[0:NB, :], AMASK[0:NB].rearrange("p a b -> p (a b)"), True, True, ["CST", "AMASK"], [pk], inc=True)
                act(DECB.rearrange("p a b -> p (a b)"), PS[b][:, 0:512], AF.Exp, [pk], ["DECB"])

            def gen_inproj(g):
                XBC, ZS = XBC2[g % 3], ZS2[g % 3]
                xk, zk = "XBC%d" % (g % 3), "ZS%d" % (g % 3)
                Wa, wka = wload(w_ssd_in[j, g, 0], 8, 384)
                Wb, wkb = wload(w_ssd_in[j, g, 1], 8, 384)
                blocks = [(Wa, wka, 0), (Wa, wka, 128), (Wa, wka, 256), (Wb, wkb, 0), (Wb, wkb, 128), (Wb, wkb, 256)]
                ccs = [2 * g, 2 * g + 1, 16 + g, 24 + g]
                for fb in range(6):
                    Wt, wk, wo = blocks[fb]
                    if fb >= 2:
                        ci = fb - 2
                        cc = ccs[ci]
                        cp(PC[:, 0:3], SCARRY[:, j, cc, :], ["SCARRY%d_%d" % (j, cc)] + HBK, ["PC"])
                        if kind == "sample":
                            S.dma("sp", lambda e, cc=cc: e.dma_start(out=PREV, in_=s_sconv[j, :, cc]), "PREVS", reads=HBK, writes=["PREVS"])
                    for ti, (off, n) in enumerate(TTS):
                        b = pbank()
                        pk = "P%d" % b
                        for kc in range(8):
                            mm(PS[b][:, 0:n], Wt[:, kc, wo:wo + 128], HB[:, kc, off:off + n], kc == 0, kc == 7, [wk, "HB%d" % ti], [pk], inc=(kc == 7))
                        if fb < 2:
                            act(TZ[:, off:off + n], PS[b][:, 0:n], AF.Tanh, [pk], ["TZ"], scale=0.5)
                            stt(ZS[:, fb, off:off + n], TZ[:, off:off + n], 1.0, PS[b][:, 0:n], ALU.add, ALU.mult, ["TZ", pk], [zk])
                        else:
                            cp(PC[:, 3 + off:3 + off + n], PS[b][:, 0:n], [pk], ["PC"], eng="act")
                        yield
                    if fb >= 2:
                        yield from conv_block(PC, "PC", CV, "CV", "scw%d" % j, 4 * cc, "scb%d" % j, cc, W_,
                                              PREV if kind == "sample" else None, "PREVS", tab=VECH)
                        cp(SCARRY[:, j, cc, :], PC[:, W_:W_ + 3], ["PC"], ["SCARRY%d_%d" % (j, cc)])
                        if kind == "sample":
                            cp(OSC[:, :, 0:2], PREV[:, :, 1:3], ["PREVS"], ["OSC"])
                            cp(OSC[:, :, 2], PC[:, 3 + SEQ:3 + TL], ["PC"], ["OSC"])
                            S.dma("sp", lambda e, cc=cc: e.dma_start(out=o_ssconv[j, :, cc], in_=OSC), "OSCD", reads=["OSC"])
                            S.dma("sp", lambda e, cc=cc: e.dma_start(out=o_psconv[j, :, cc], in_=PC[:, SEQ:SEQ + 3]), "OPSC", reads=["PC"])
                        if kind == "none":
                            S.op("dve", lambda e: e.memset(CV[:, SEQ:TL], 0.0), ["CV"], ["CV"])
                        act(TZ[:, :], CV, AF.Tanh, ["CV"], ["TZ"])
                        yield
                        stt(XBC[:, ci, :], TZ[:, :], 1.0, CV, ALU.add, ALU.mult, ["TZ", "CV"], [xk])
                        yield

            state_ready = {}

            def gen_chunk(g, ch, s_):
                d = TS_[s_]
                XTM, XDT, XDTE, ZTM, BTM, CBM, AL, ED, SC, Y1, Y2, JK, Y3, SS = (d[n] for n in
                    ("XTM", "XDT", "XDTE", "ZTM", "BTM", "CBM", "AL", "ED", "SC", "Y1", "Y2", "JK", "Y3", "SS"))
                K = lambda n: "%s_%d" % (n, s_)
                XBC, ZS = XBC2[g % 3], ZS2[g % 3]
                xk, zk = "XBC%d" % (g % 3), "ZS%d" % (g % 3)
                gs = slice(4 * g, 4 * g + 4)
                c0, Q = chunks[ch]
                decode = (ch == 4 and kind == "sample")
                cols = slice(c0, c0 + Q)
                r4 = lambda ap: ap.rearrange("p (a b) -> p a b", a=4)
                pk1 = "PBf"
                pv = PB[:, 0:640]
                for i in range(2):
                    tr(pv[0:Q, i * 128:(i + 1) * 128], XBC[:, i, cols], IDB[:], [xk, "IDB"], [pk1], inc=False)
                for i in range(2):
                    tr(pv[0:Q, 256 + i * 128:256 + (i + 1) * 128], ZS[:, i, cols], IDB[:], [zk, "IDB"], [pk1], inc=False)
                tr(pv[0:Q, 512:640], XBC[:, 2, cols], IDB[:], [xk, "IDB"], [pk1], inc=True)
                cp(XTM[0:Q, :], pv[0:Q, 0:256], [pk1], [K("XTM")], eng="act")
                act(ZTM[0:Q, :], pv[0:Q, 256:512], AF.Copy, [pk1], [K("ZTM")], scale=0.5)
                cp(BTM[0:Q, :], pv[0:Q, 512:640], [pk1], [K("BTM")], eng="act")
                yield
                tt(r4(XDT[0:Q, :]), r4(XTM[0:Q, :]), DTM[0:Q, ch, gs].unsqueeze(2).to_broadcast([Q, 4, 64]), ALU.mult, [K("XTM"), "DTM"], [K("XDT")])
                yield
                if not decode:
                    tt(r4(XDTE[0:Q, :]), r4(XDT[0:Q, :]), DTE[0:Q, ch, gs].unsqueeze(2).to_broadcast([Q, 4, 64]), ALU.mult, [K("XDT"), "DTE"], [K("XDTE")])
                    yield
                    while state_ready.get(g, 0) != ch:
                        yield
                    bO = pbank()
                    pkO = "P%d" % bO
                    mm(PS[bO][0:Q, 0:256], XBC[:, 3, cols], HTB[:, g * 256:(g + 1) * 256], True, True, [xk, hbg(g)], [pkO], inc=True)
                    tt(r4(Y1[0:Q, :]), r4(PS[bO][0:Q, 0:256]), ECS[0:Q, ch, gs].unsqueeze(2).to_broadcast([Q, 4, 64]), ALU.mult, [pkO, "ECS"], [K("Y1")])
                    yield
                    htg = HT[:, j, g * 256:(g + 1) * 256]
                    tt(r4(htg), r4(htg), CD[:, ch, gs].unsqueeze(2).to_broadcast([128, 4, 64]), ALU.mult, [hkg(g), "CD"], [hkg(g)])
                    b4 = pbank()
                    pk4 = "P%d" % b4
                    mm(PS[b4][:, 0:256], BTM[0:Q, :], XDTE[0:Q, :], True, True, [K("BTM"), K("XDTE")], [pk4], inc=True)
                    tt(htg, htg, PS[b4][:, 0:256], ALU.add, [hkg(g), pk4], [hkg(g)])
                    cp(HTB[:, g * 256:(g + 1) * 256], htg, [hkg(g)], [hbg(g)], eng="act")
                    state_ready[g] = ch + 1
                    yield
                    b2 = pbank()
                    pk2 = "P%d" % b2
                    mm(PS[b2][0:Q, 0:Q], XBC[:, 2, cols], XBC[:, 3, cols], True, True, [xk], [pk2], inc=True)
                    tt(CBM[0:Q, 0:Q], PS[b2][0:Q, 0:Q], VM[0:Q, 0:Q], ALU.mult, [pk2, "CST"], [K("CBM")])
                    yield
                    tt(AL[0:Q, :, 0:Q], MLOW[0:Q, 0:Q].unsqueeze(1).to_broadcast([Q, 4, Q]),
                       ATM[0:Q, ch, gs].unsqueeze(2).to_broadcast([Q, 4, Q]), ALU.mult, ["CST", "ATM"], [K("AL")])
                    yield
                    b3 = pbank()
                    pk3 = "P%d" % b3
                    for h4 in range(4):
                        mm(PS[b3][0:Q, h4 * 128:h4 * 128 + Q], AL[0:Q, h4, 0:Q], VM[0:Q, 0:Q], True, True, [K("AL"), "CST"], [pk3], inc=(h4 == 3))
                    act(ED[0:Q, :, 0:Q], PS[b3][0:Q, :].rearrange("p (a b) -> p a b", a=4)[:, :, 0:Q], AF.Exp, [pk3], [K("ED")])
                    yield
                    tt(SC[0:Q, :, 0:Q], ED[0:Q, :, 0:Q], CBM[0:Q, 0:Q].unsqueeze(1).to_broadcast([Q, 4, Q]), ALU.mult, [K("ED"), K("CBM")], [K("SC")])
                    yield
                    XD = JK.bitcast(BF16)[:, 0:256]
                    tt(r4(XD[0:Q, :]), r4(XTM[0:Q, :]), dskb(j, g)[0:Q, :].unsqueeze(2).to_broadcast([Q, 4, 64]), ALU.mult, [K("XTM"), "ALOGB"], [K("JK")])
                    yield
                    bY = pbank()
                    pkY = "P%d" % bY
                    mm(PS[bY][0:Q, 0:256], IDB[0:Q, 0:Q], XD[0:Q, :], True, False, ["IDB", K("JK")], [pkY], inc=False)
                    for h4 in range(4):
                        mm(PS[bY][0:Q, h4 * 64:(h4 + 1) * 64], SC[0:Q, h4, 0:Q], XDT[0:Q, h4 * 64:(h4 + 1) * 64], False, h4 == 3,
                           [K("SC"), K("XDT")], [pkY], inc=(h4 == 3))
                    tt(Y1[0:Q, :], Y1[0:Q, :], PS[bY][0:Q, 0:256], ALU.add, [K("Y1"), pkY], [K("Y1")])
                    yield
                else:
                    bY = pbank()
                    pkY = "P%d" % bY
                    tt(CMASK[:, :, :], XBC[:, 3, cols].unsqueeze(2).to_broadcast([128, NB, NB]),
                       I16B.rearrange("p a b -> p (a b)").rearrange("p (a b) -> p a b", a=NB), ALU.mult, [xk, "CST"], ["CMASK"])
                    tt(BMASK[0:NB], BTM[0:NB, :].unsqueeze(1).to_broadcast([NB, NB, 128]),
                       IDF[0:NB, 0:NB].unsqueeze(2).to_broadcast([NB, NB, 128]), ALU.mult, [K("BTM"), "CST"], ["BMASK"])
                    for bi in range(NB):
                        sl = bi % 2
                        sk, nk = "SIN%d" % sl, "SNEW%d" % sl
                        S.dma("sp", lambda e, bi=bi, sl=sl: e.dma_start(out=SIN[:, sl, :], in_=s_sh[j, bi, :, g * 256:(g + 1) * 256]),
                              sk, reads=HBK, writes=[sk])
                        tt(r4(SNEW[:, sl, :]), r4(SIN[:, sl, :]), DECB[:, bi, gs].unsqueeze(2).to_broadcast([128, 4, 64]), ALU.mult, [sk, "DECB"], [nk])
                        b5 = pbank()
                        if b5 == bY:
                            b5 = pbank()
                        pk5 = "P%d" % b5
                        mm(PS[b5][:, 0:256], BMASK[0:NB, bi, :], XDT[0:NB, :], True, True, ["BMASK", K("XDT")], [pk5], inc=True)
                        tt(SNEW[:, sl, :], SNEW[:, sl, :], PS[b5][:, 0:256], ALU.add, [nk, pk5], [nk])
                        S.dma("sp", lambda e, bi=bi, sl=sl: e.dma_start(out=o_ssh[j, bi, :, g * 256:(g + 1) * 256], in_=SNEW[:, sl, :]),
                              "O" + nk, reads=[nk])
                        mm(PS[bY][0:NB, 0:256], CMASK[:, bi, :], SNEW[:, sl, :], bi == 0, bi == NB - 1, ["CMASK", nk], [pkY], inc=True)
                    cp(Y1[0:Q, :], PS[bY][0:Q, 0:256], [pkY], [K("Y1")])
                    yield
                    tt(r4(JK[0:Q, :]), r4(XTM[0:Q, :]), dskb(j, g)[0:Q, :].unsqueeze(2).to_broadcast([Q, 4, 64]), ALU.mult, [K("XTM"), "ALOGB"], [K("JK")])
                    yield
                    tt(Y1[0:Q, :], Y1[0:Q, :], JK[0:Q, :], ALU.add, [K("Y1"), K("JK")], [K("Y1")])
                    yield
                tt(Y2[0:Q, :], Y1[0:Q, :], ZTM[0:Q, :], ALU.mult, [K("Y1"), K("ZTM")], [K("Y2")])
                yield
                tt(JK[0:Q, :], Y2[0:Q, :], Y2[0:Q, :], ALU.mult, [K("Y2"), K("JK")], [K("JK")])
                yield
                S.op("dve", lambda e: e.tensor_reduce(out=SS[0:Q, 0:1], in_=JK[0:Q, :], axis=AX.X, op=ALU.add), [K("JK")], [K("SS")])
                yield
                ts(SS[0:Q, 0:1], SS[0:Q, 0:1], 1.0 / 256, EPS, ALU.mult, ALU.add, [K("SS")], [K("SS")], eng="pool")
                tt(SS[0:Q, 0:1], SS[0:Q, 0:1], NHALF[0:Q, 0:1], ALU.pow, [K("SS"), "NHALF"], [K("SS")], eng="pool")
                yield
                act(Y3[0:Q, :], Y2[0:Q, :], AF.Identity, [K("Y2"), K("SS")], [K("Y3")], scale=SS[0:Q, 0:1])
                yield
                pk6 = "PBb"
                pv6 = PB2[:, 0:256]
                for i in range(2):
                    tr(pv6[:, i * 128:i * 128 + Q], Y3[0:Q, i * 128:(i + 1) * 128], IDB[0:Q, 0:Q], [K("Y3"), "IDB"], [pk6], inc=(i == 1))
                for i in range(2):
                    act(YF[:, 2 * g + i, cols], pv6[:, i * 128:i * 128 + Q], AF.Identity, [pk6, "VEC"], ["YF%d" % (2 * g + i)],
                        scale=vcol("snorm%d" % j, 2 * g + i))
                yield

            run_all(gen_inproj(0))
            inproj_done = {0: True}
            spawned = {0}

            def wrap_inproj(g):
                yield from gen_inproj(g)
                inproj_done[g] = True

            pending = [(g, ch) for g in range(8) for ch in range(ncht)]
            active = []
            free = list(range(NSL))
            extra = []
            xsteps = CFG.get("extra_steps", 1)
            while pending or active or extra:
                while pending and free and inproj_done.get(pending[0][0]):
                    g, ch = pending.pop(0)
                    s_ = free.pop(0)
                    active.append((gen_chunk(g, ch, s_), s_))
                    if g + 1 < 8 and (g + 1) not in spawned:
                        spawned.add(g + 1)
                        extra.append(wrap_inproj(g + 1))
                for item in list(active):
                    try:
                        next(item[0])
                    except StopIteration:
                        active.remove(item)
                        free.append(item[1])
                for gen in list(extra):
                    try:
                        for _ in range(xsteps):
                            next(gen)
                    except StopIteration:
                        extra.remove(gen)
            if kind == "none":
                for c16 in range(16):
                    S.op("dve", lambda e, c16=c16: e.memset(YF[:, c16, SEQ:TL], 0.0), ["YF%d" % c16], ["YF%d" % c16])
            if part == NPART - 1:
                S.dma("sp", lambda e: e.dma_start(out=o_psh[j], in_=HT[:, j, :]), "OPSH%d" % j, reads=allhk)
            for q in range(4):
                Wt, wk = wload(w_ssd_out[j, q], 16, 256)
                for cc2 in range(2):
                    c = 2 * q + cc2
                    for ti, (off, n) in enumerate(TTS):
                        b = pbank()
                        pk = "P%d" % b
                        for kc in range(16):
                            mm(PS[b][:, 0:n], Wt[:, kc, cc2 * 128:(cc2 + 1) * 128], YF[:, kc, off:off + n], kc == 0, kc == 15,
                               [wk, "YF%d" % kc], [pk], inc=(kc == 15))
                        cp(M[:, c, off:off + n], PS[b][:, 0:n], [pk], ["M%d" % ti], eng="act")

        XK = ["X0", "X1"]
        for part in CFG["parts"]:
            S.dma("sp", lambda e, part=part: e.dma_start(out=X[:], in_=xin[:, :, part, :]), "XLD", writes=XK)
            for layer in CFG["layers"]:
                j = layer // 2
                if CFG["mixer"]:
                    prenorm("mpre%d" % layer, None)
                    if layer % 2 == 0:
                        ssd(j, part)
                    else:
                        lru(j, part)
                    postnorm("mpost%d" % layer)
                if CFG["ffn"]:
                    prenorm("fpre%d" % layer, None)
                    ffn(layer)
                    postnorm("fpost%d" % layer)
            S.dma("sp", lambda e, part=part: e.dma_start(out=yout[:, :, part, :], in_=X[:]), "XST", reads=XK)
        S.finish("sp")
        S.emit()
        build_nc.stats = dict(S.count), {k: len(v) for k, v in S.ops.items()}, dict(S.dcount)
    return nc


_NC_CACHE = {}


def _consts():
    c = np.zeros((128, 6, 128), np.float32)
    c[:, 0] = np.eye(128)
    t = np.arange(128)
    c[:, 1] = (t[:, None] <= t[None, :])
    c[:, 2] = (t[:, None] > t[None, :])
    c[:, 3] = 1.0
    i16 = np.eye(16, dtype=np.float32).reshape(-1)
    c[:, 4:6] = np.broadcast_to(i16.reshape(1, 2, 128), (128, 2, 128))
    return c


def kernel(x_prompt, x_sample, state_ssd_conv, state_ssd_h, state_lru_conv, state_lru_h, meta_tokens,
           norm_mix_pre, norm_mix_post, norm_ffn_pre, norm_ffn_post,
           ssd_w_in, ssd_conv_w, ssd_conv_b, ssd_dt_bias, ssd_a_log, ssd_d, ssd_norm, ssd_w_out,
           lru_w_in, lru_b_in, lru_conv_w, lru_conv_b, lru_w_a, lru_b_a, lru_w_x, lru_b_x, lru_lambda,
           lru_w_out, lru_b_out, ffn_w1, ffn_w2):
    f = lambda a: np.asarray(a, dtype=np.float32)
    x_prompt, x_sample = f(x_prompt), f(x_sample)
    NCORE = 8
    vm = VecMap()
    norms = {"mpre": f(norm_mix_pre), "mpost": f(norm_mix_post), "fpre": f(norm_ffn_pre), "fpost": f(norm_ffn_post)}
    for i in range(4):
        for kind in ("mpre", "mpost", "fpre", "fpost"):
            vm.add("%s%d" % (kind, i), fm(norms[kind][i]))
    for j in range(2):
        cw = f(ssd_conv_w)[j]
        vm.add("scw%d" % j, cw.reshape(4, 32, 128).transpose(2, 1, 0).reshape(128, 128))
        vm.add("scb%d" % j, fm(f(ssd_conv_b)[j]))
        vm.add("snorm%d" % j, fm(f(ssd_norm)[j]))
        dtb = np.zeros((128, 1), np.float32)
        dtb[0:32, 0] = f(ssd_dt_bias)[j]
        vm.add("sdt%d" % j, dtb)
        vm.add("lbin%d" % j, fm(f(lru_b_in)[j]))
        lw = f(lru_conv_w)[j]
        vm.add("lcw%d" % j, lw.reshape(4, 8, 128).transpose(2, 1, 0).reshape(128, 32))
        vm.add("lcb%d" % j, fm(f(lru_conv_b)[j]))
        vm.add("lba%d" % j, fm(f(lru_b_a)[j]))
        vm.add("lbx%d" % j, fm(f(lru_b_x)[j]))
        vm.add("llam%d" % j, fm(f(lru_lambda)[j]))
        vm.add("lbout%d" % j, fm(f(lru_b_out)[j]))
    assert vm.cols == VCOLS and vm.n == NV
    vec = vm.table()
    vecb = np.zeros((1, 128), np.float32)
    for j in range(2):
        vecb[0, j * 64:j * 64 + 32] = f(ssd_a_log)[j]
        vecb[0, j * 64 + 32:j * 64 + 64] = f(ssd_d)[j]

    def kp(w):
        K, N = w.shape
        return np.ascontiguousarray(w.reshape(K // 128, 128, N).transpose(1, 0, 2))

    DI = 2048
    w_ssd_in = np.zeros((2, 8, 2, 128, 8, 384), np.float32)
    w_ssd_dt = np.zeros((2, 128, 8, 32), np.float32)
    w_ssd_out = np.zeros((2, 4, 128, 16, 256), np.float32)
    for j in range(2):
        w = f(ssd_w_in)[j]
        for g in range(8):
            cols = np.concatenate([
                np.arange(256 * g, 256 * g + 256),
                DI + np.arange(256 * g, 256 * g + 256),
                DI + 2048 + np.arange(128 * g, 128 * g + 128),
                DI + 3072 + np.arange(128 * g, 128 * g + 128)])
            blk = kp(w[:, cols])
            w_ssd_in[j, g, 0] = blk[:, :, 0:384]
            w_ssd_in[j, g, 1] = blk[:, :, 384:768]
        w_ssd_dt[j] = kp(w[:, DI + 4096:DI + 4096 + 32])
        wo = kp(f(ssd_w_out)[j])
        for q in range(4):
            w_ssd_out[j, q] = wo[:, :, q * 256:(q + 1) * 256]
    w_lru_in = np.zeros((2, 8, 128, 8, 256), np.float32)
    w_lru_ax = np.zeros((2, 128, 8, 256), np.float32)
    w_lru_out = np.zeros((2, 2, 128, 8, 512), np.float32)
    for j in range(2):
        w = kp(f(lru_w_in)[j])
        for k in range(8):
            w_lru_in[j, k, :, :, 0:128] = w[:, :, k * 128:(k + 1) * 128]
            w_lru_in[j, k, :, :, 128:256] = w[:, :, 1024 + k * 128:1024 + (k + 1) * 128]
            w_lru_ax[j, :, k, 0:128] = f(lru_w_a)[j, k]
            w_lru_ax[j, :, k, 128:256] = f(lru_w_x)[j, k]
        wo = kp(f(lru_w_out)[j])
        for q in range(2):
            w_lru_out[j, q] = wo[:, :, q * 512:(q + 1) * 512]
    w_ffn1 = np.zeros((4, 8, 128, 8, 512), np.float32)
    w_ffn2 = np.zeros((4, 8, 128, 32, 128), np.float32)
    for i in range(4):
        w1 = kp(f(ffn_w1)[i])
        w2 = kp(f(ffn_w2)[i])
        for q in range(8):
            w_ffn1[i, q] = w1[:, :, q * 512:(q + 1) * 512]
            w_ffn2[i, q] = w2[:, :, q * 128:(q + 1) * 128]
    cst = _consts()

    in_maps = []
    meta = f(meta_tokens)
    ssc, ssh, slc, slh = f(state_ssd_conv), f(state_ssd_h), f(state_lru_conv), f(state_lru_h)
    for c in range(NCORE):
        seq = np.concatenate([meta, x_prompt[c]], axis=0)
        cols = np.zeros((NPART, TL, D), np.float32)
        cols[0] = seq[0:528]
        cols[1, 0:512] = seq[528:1040]
        cols[2, 0:512] = seq[1040:1552]
        cols[3, 0:512] = seq[1552:2064]
        cols[3, 512:528] = x_sample[c * NB:(c + 1) * NB, 0]
        xin = np.ascontiguousarray(cols.reshape(NPART, TL, 8, 128).transpose(3, 2, 0, 1))
        bs = slice(c * NB, (c + 1) * NB)
        s_sconv = np.ascontiguousarray(ssc[:, bs].reshape(2, NB, 3, 32, 128).transpose(0, 4, 3, 1, 2))
        s_sh = np.ascontiguousarray(ssh[:, bs].reshape(2, NB, 2048, 128).transpose(0, 1, 3, 2))
        s_lconv = np.ascontiguousarray(slc[:, bs].reshape(2, NB, 3, 8, 128).transpose(0, 4, 3, 1, 2))
        s_lh = np.ascontiguousarray(slh[:, bs].reshape(2, NB, 8, 128).transpose(0, 3, 2, 1))
        in_maps.append(dict(xin=xin, cst=cst, vec=vec, vecb=vecb, w_ssd_in=w_ssd_in, w_ssd_dt=w_ssd_dt, w_ssd_out=w_ssd_out,
                            w_lru_in=w_lru_in, w_lru_ax=w_lru_ax, w_lru_out=w_lru_out, w_ffn1=w_ffn1, w_ffn2=w_ffn2,
                            s_sconv=s_sconv, s_sh=s_sh, s_lconv=s_lconv, s_lh=s_lh))
    if CFG["ncore"] < NCORE:
        in_maps = in_maps[:CFG["ncore"]]
    if "nc" not in _NC_CACHE:
        _NC_CACHE["nc"] = build_nc()
    nc = _NC_CACHE["nc"]
    if CFG.get("trace"):
        res = run_bass_kernel_spmd(nc, in_maps, core_ids=list(range(len(in_maps))), trace=True)
        print("EXEC_TIME_NS", res.exec_time_ns)
    else:
        res = run_bass_kernel_spmd(nc, in_maps, core_ids=list(range(len(in_maps))))
    R = list(res.results)
    while len(R) < NCORE:
        R.append(R[0])

    y_prompt = np.zeros((8, 2048, D), np.float32)
    y_sample = np.zeros((128, 1, D), np.float32)
    p_ssd_conv = np.zeros((2, 8, 3, 4096), np.float32)
    p_ssd_h = np.zeros((2, 8, 32, 64, 128), np.float32)
    p_lru_conv = np.zeros((2, 8, 3, 1024), np.float32)
    p_lru_h = np.zeros((2, 8, 1024), np.float32)
    s_ssd_conv = np.zeros((2, 128, 3, 4096), np.float32)
    s_ssd_h = np.zeros((2, 128, 32, 64, 128), np.float32)
    s_lru_conv = np.zeros((2, 128, 3, 1024), np.float32)
    s_lru_h = np.zeros((2, 128, 1024), np.float32)
    for c in range(NCORE):
        r = R[c]
        yo = np.asarray(r["yout"]).transpose(2, 3, 1, 0).reshape(NPART, TL, D)
        seq = np.concatenate([yo[0, 0:528], yo[1, 0:512], yo[2, 0:512], yo[3, 0:512]], axis=0)
        y_prompt[c] = seq[16:]
        bs = slice(c * NB, (c + 1) * NB)
        y_sample[bs, 0] = yo[3, 512:528]
        p_ssd_conv[:, c] = np.asarray(r["o_psconv"]).transpose(0, 3, 2, 1).reshape(2, 3, 4096)
        s_ssd_conv[:, bs] = np.asarray(r["o_ssconv"]).transpose(0, 3, 4, 2, 1).reshape(2, NB, 3, 4096)
        p_ssd_h[:, c] = np.asarray(r["o_psh"]).transpose(0, 2, 1).reshape(2, 32, 64, 128)
        s_ssd_h[:, bs] = np.asarray(r["o_ssh"]).transpose(0, 1, 3, 2).reshape(2, NB, 32, 64, 128)
        p_lru_conv[:, c] = np.asarray(r["o_plconv"]).transpose(0, 3, 2, 1).reshape(2, 3, 1024)
        s_lru_conv[:, bs] = np.asarray(r["o_slconv"]).transpose(0, 3, 4, 2, 1).reshape(2, NB, 3, 1024)
        p_lru_h[:, c] = np.asarray(r["o_plh"]).reshape(2, 1024)
        s_lru_h[:, bs] = np.asarray(r["o_slh"]).transpose(0, 3, 2, 1).reshape(2, NB, 1024)
    return (y_prompt, y_sample, p_ssd_conv, p_ssd_h, p_lru_conv, p_lru_h,
            s_ssd_conv, s_ssd_h, s_lru_conv, s_lru_h)
```

```python
import numpy as np
from contextlib import ExitStack
import concourse.bass as bass
import concourse.mybir as mybir
from concourse.bass_utils import run_bass_kernel_spmd

F32 = mybir.dt.float32
BF16 = mybir.dt.bfloat16
AF = mybir.ActivationFunctionType
ALU = mybir.AluOpType
AX = mybir.AxisListType
ENGS = ["pe", "act", "dve", "pool", "sp"]

D = 1024
NPART = 4
TL = 528
SEQ = 512
NTAIL = 16
TTS = [(0, 264), (264, 264)]
EPS = 1e-6
NB = 16


class Sched:
    def __init__(self, nc):
        self.nc = nc
        self.ops = {e: [] for e in ENGS}
        self.count = {e: 0 for e in ENGS}
        self.sem = {e: nc.alloc_semaphore(name="s_" + e) for e in ENGS}
        self.seen = {e: {} for e in ENGS}
        self.last_w = {}
        self.readers = {}
        self.dsem = {}
        self.dcount = {}
        self.pending_pe = False

    def _semh(self, key):
        return self.sem[key] if key in self.sem else self.dsem[key]

    def _deps(self, eng, reads, writes):
        deps = []
        for b in reads:
            if b in self.last_w:
                deps.append(self.last_w[b])
        for b in writes:
            if b in self.last_w:
                deps.append(self.last_w[b])
            deps.extend(self.readers.get(b, ()))
        waits = {}
        for (k, v) in deps:
            if k == eng and eng == "pe":
                continue
            if self.seen[eng].get(k, 0) >= v:
                continue
            if waits.get(k, 0) < v:
                waits[k] = v
        for k, v in waits.items():
            self.seen[eng][k] = v
        return waits

    def _record(self, ev, reads, writes):
        for b in reads:
            self.readers.setdefault(b, []).append(ev)
        for b in writes:
            self.last_w[b] = ev
            self.readers[b] = []

    def op(self, eng, fn, reads=(), writes=(), inc=True):
        waits = self._deps(eng, reads, writes)
        ev = (eng, self.count[eng] + 1)
        if inc:
            self.count[eng] += 1
            if eng == "pe":
                self.pending_pe = False
        else:
            self.pending_pe = True
        self.ops[eng].append((waits, fn, eng if inc else None, 1))
        self._record(ev, reads, writes)

    def dma(self, queue, fn, key, reads=(), writes=()):
        if key not in self.dsem:
            self.dsem[key] = self.nc.alloc_semaphore(name="d_" + str(key))
            self.dcount[key] = 0
        waits = self._deps(queue, reads, writes)
        self.dcount[key] += 16
        ev = (key, self.dcount[key])
        self.ops[queue].append((waits, fn, key, 16))
        self._record(ev, reads, writes)

    def finish(self, queue="sp"):
        waits = dict(self.dcount)
        for e in ENGS:
            if e != queue and self.count[e] > 0:
                waits[e] = self.count[e]
        self.ops[queue].append((waits, None, None, 0))

    def emit(self):
        nc = self.nc
        assert not self.pending_pe
        engobj = {"pe": "tensor", "act": "scalar", "dve": "vector", "pool": "gpsimd", "sp": "sync"}
        with nc.Block() as block:
            for e in ENGS:
                ops = self.ops[e]

                def body(eo, ops=ops):
                    for (waits, fn, inc_key, amt) in ops:
                        for k, v in waits.items():
                            eo.wait_ge(self._semh(k), v)
                        if fn is None:
                            continue
                        ins = fn(eo)
                        if inc_key is not None:
                            ins.then_inc(self._semh(inc_key), amt)

                getattr(block, engobj[e])(body)


class VecMap:
    def __init__(self):
        self.cols = {}
        self.data = []
        self.n = 0

    def add(self, name, arr):
        arr = np.ascontiguousarray(arr, dtype=np.float32).reshape(128, -1)
        self.cols[name] = self.n
        self.data.append(arr)
        self.n += arr.shape[1]

    def table(self):
        return np.concatenate(self.data, axis=1)


def fm(v):
    return np.ascontiguousarray(np.asarray(v, dtype=np.float32).reshape(-1, 128).T)


def build_vec_layout():
    cols = {}
    n = 0

    def add(name, k):
        nonlocal n
        cols[name] = n
        n += k
    for i in range(4):
        for kind in ("mpre", "mpost", "fpre", "fpost"):
            add("%s%d" % (kind, i), 8)
    for j in range(2):
        add("scw%d" % j, 128)
        add("scb%d" % j, 32)
        add("snorm%d" % j, 16)
        add("sdt%d" % j, 1)
        add("lbin%d" % j, 16)
        add("lcw%d" % j, 32)
        add("lcb%d" % j, 8)
        add("lba%d" % j, 8)
        add("lbx%d" % j, 8)
        add("llam%d" % j, 8)
        add("lbout%d" % j, 8)
    return cols, n


VCOLS, NV = build_vec_layout()
CFG = dict(parts=(0, 1, 2, 3), layers=(0, 1, 2, 3), mixer=True, ffn=True, ncore=8, ssd_stop=99, groups=8, nchunks=99)


def build_nc():
    nc = bass.Bass("TRN2", target_bir_lowering=False)

    def din(name, shape):
        return nc.dram_tensor(name, list(shape), F32, kind="ExternalInput").ap()

    def dout(name, shape):
        return nc.dram_tensor(name, list(shape), F32, kind="ExternalOutput").ap()

    xin = din("xin", [128, 8, NPART, TL])
    cst = din("cst", [128, 6, 128])
    vec = din("vec", [128, NV])
    vecb = din("vecb", [1, 2 * 64])
    w_ssd_in = din("w_ssd_in", [2, 8, 2, 128, 8, 384])
    w_ssd_dt = din("w_ssd_dt", [2, 128, 8, 32])
    w_ssd_out = din("w_ssd_out", [2, 4, 128, 16, 256])
    w_lru_in = din("w_lru_in", [2, 8, 128, 8, 256])
    w_lru_ax = din("w_lru_ax", [2, 128, 8, 256])
    w_lru_out = din("w_lru_out", [2, 2, 128, 8, 512])
    w_ffn1 = din("w_ffn1", [4, 8, 128, 8, 512])
    w_ffn2 = din("w_ffn2", [4, 8, 128, 32, 128])
    s_sconv = din("s_sconv", [2, 128, 32, NB, 3])
    s_sh = din("s_sh", [2, NB, 128, 2048])
    s_lconv = din("s_lconv", [2, 128, 8, NB, 3])
    s_lh = din("s_lh", [2, 128, 8, NB])

    yout = dout("yout", [128, 8, NPART, TL])
    o_psconv = dout("o_psconv", [2, 128, 32, 3])
    o_ssconv = dout("o_ssconv", [2, 128, 32, NB, 3])
    o_psh = dout("o_psh", [2, 128, 2048])
    o_ssh = dout("o_ssh", [2, NB, 128, 2048])
    o_plconv = dout("o_plconv", [2, 128, 8, 3])
    o_slconv = dout("o_slconv", [2, 128, 8, NB, 3])
    o_plh = dout("o_plh", [2, 8, 128, 1])
    o_slh = dout("o_slh", [2, 128, 8, NB])

    es = ExitStack()
    with es:
        def sb(name, shape, dt=F32):
            return es.enter_context(nc.sbuf_tensor(name, list(shape), dt))

        S = Sched(nc)
        X = sb("X", [128, 8, TL])
        M = sb("M", [128, 8, TL])
        HB = sb("HB", [128, 8, TL], BF16)
        HT = sb("HT", [128, 2, 2048])
        HTB = sb("HTB", [128, 2048], BF16)
        NRING = 4
        RING = [sb("ring%d" % i, [128, 4096], BF16) for i in range(NRING)]
        CST = sb("CST", [128, 6, 128])
        IDB = sb("IDB", [128, 128], BF16)
        ONEB = sb("ONEB", [128, 128], BF16)
        VEC = sb("VEC", [128, NV])
        ALOGB = sb("ALOGB", [128, 128])
        ANEG = sb("ANEG", [128, 2, 32])
        LSC = sb("LSC", [128, 2, 8])
        SQ = sb("SQ", [128, 8, 264], BF16)
        RS = sb("RS", [128, 2, 264])
        WDT = sb("WDT", [128, 8, 32], BF16)
        WAX = sb("WAX", [128, 8, 256], BF16)
        SCARRY = sb("SCARRY", [128, 2, 32, 3])
        LCARRY = sb("LCARRY", [128, 2, 8, 3])
        HCARRY = sb("HCARRY", [128, 2, 8])
        ARENA_F32 = 23400
        ARENA = sb("ARENA", [128, ARENA_F32])
        PS = [es.enter_context(nc.psum_tensor("ps%d" % i, [128, 512], F32)) for i in range(6)]
        PB = es.enter_context(nc.psum_tensor("psb", [128, 1024], BF16))
        PB2 = es.enter_context(nc.psum_tensor("psb2", [128, 1024], BF16))

        IDF = CST[:, 0, :]
        VM = CST[:, 1, :]
        MLOW = CST[:, 2, :]
        ONEF = CST[:, 3, :]
        I16B = CST[:, 4:6, :]

        st = {"bank": 0, "ring": 0, "uid": 0}

        def pbank():
            b = st["bank"]
            st["bank"] = (b + 1) % CFG.get("nbanks", 6)
            return b

        def vcol(name, off=0, n=1):
            c = VCOLS[name] + off
            return VEC[:, c:c + n]

        def act(out, in_, func, reads, writes, **kw):
            S.op("act", lambda e: e.activation(out=out, in_=in_, func=func, **kw), reads, writes)

        def tt(out, in0, in1, op, reads, writes, eng="dve"):
            S.op(eng, lambda e: e.tensor_tensor(out=out, in0=in0, in1=in1, op=op), reads, writes)

        def ts(out, in0, s1, s2, op0, op1, reads, writes, eng="dve"):
            if op1 is None:
                S.op(eng, lambda e: e.tensor_scalar(out=out, in0=in0, scalar1=s1, scalar2=None, op0=op0), reads, writes)
            else:
                S.op(eng, lambda e: e.tensor_scalar(out=out, in0=in0, scalar1=s1, scalar2=s2, op0=op0, op1=op1), reads, writes)

        def stt(out, in0, scalar, in1, op0, op1, reads, writes):
            S.op("dve", lambda e: e.scalar_tensor_tensor(out=out, in0=in0, scalar=scalar, in1=in1, op0=op0, op1=op1), reads, writes)

        def mm(out, lhsT, rhs, start, stop, reads, writes, inc):
            S.op("pe", lambda e: e.matmul(out, lhsT=lhsT, rhs=rhs, start=start, stop=stop), reads, writes, inc=inc)

        def tr(out, in_, ident, reads, writes, inc=True):
            S.op("pe", lambda e: e.transpose(out, in_, ident), reads, writes, inc=inc)

        def cp(out, in_, reads, writes, eng="dve"):
            if eng == "act":
                act(out, in_, AF.Copy, reads, writes)
            else:
                S.op(eng, lambda e: e.tensor_copy(out=out, in_=in_), reads, writes)

        def wload(src_ap, n1, n2):
            i = st["ring"]
            st["ring"] = (i + 1) % NRING
            view = RING[i][:, 0:n1 * n2].rearrange("p (a b) -> p a b", a=n1)
            key = "ring%d" % i
            S.dma("pool", lambda e: e.dma_start(out=view, in_=src_ap), key, writes=[key])
            return view, key

        S.dma("sp", lambda e: e.dma_start(out=CST[:], in_=cst), "CST", writes=["CST"])
        S.dma("sp", lambda e: e.dma_start(out=VEC[:], in_=vec), "VEC", writes=["VEC"])
        S.dma("sp", lambda e: e.dma_start(out=ALOGB[:], in_=vecb.partition_broadcast(128)), "ALOGB", writes=["ALOGB"])
        cp(IDB[:], IDF, ["CST"], ["IDB"])
        cp(ONEB[:], ONEF, ["CST"], ["ONEB"])
        for j in range(2):
            act(ANEG[:, j, :], ALOGB[:, j * 64:j * 64 + 32], AF.Exp, ["ALOGB"], ["ANEG"])
        ts(ANEG[:], ANEG[:], -1.0, None, ALU.mult, None, ["ANEG"], ["ANEG"])
        for j in range(2):
            act(LSC[:, j, :], vcol("llam%d" % j, 0, 8), AF.Exp, ["VEC"], ["LSC"], scale=-1.0)
        act(LSC[:], LSC[:], AF.Ln, ["LSC"], ["LSC"], bias=1.0)
        ts(LSC[:], LSC[:], -8.0, None, ALU.mult, None, ["LSC"], ["LSC"])
        VECH = sb("VECH", [128, NV])
        ts(VECH[:], VEC[:], 0.5, None, ALU.mult, None, ["VEC"], ["VECH"])
        NHALF = sb("NHALF", [128, 2])
        S.op("dve", lambda e: e.memset(NHALF[:], -0.5), [], ["NHALF"])
        S.op("dve", lambda e: e.memset(SCARRY[:], 0.0), [], ["SCARRY%d_%d" % (a_, b_) for a_ in range(2) for b_ in range(32)])
        S.op("dve", lambda e: e.memset(LCARRY[:], 0.0), [], ["LCARRY%d_%d" % (a_, b_) for a_ in range(2) for b_ in range(8)])
        S.op("dve", lambda e: e.memset(HCARRY[:], 0.0), [], ["HCARRY%d_%d" % (a_, b_) for a_ in range(2) for b_ in range(8)])
        S.op("dve", lambda e: e.memset(HT[:], 0.0), [], ["HT%d_%d" % (jj, gg) for jj in range(2) for gg in range(8)])

        def dskb(j, g):
            return ALOGB[:, j * 64 + 32 + 4 * g: j * 64 + 32 + 4 * g + 4]

        def rstd_for(src, srckeys, ti):
            off, n = TTS[ti]
            b = pbank()
            pk = "P%d" % b
            for kc in range(8):
                sq = SQ[:, kc, 0:n]
                sk = "SQ%d" % kc
                act(sq, src[:, kc, off:off + n], AF.Square, srckeys, [sk])
                mm(PS[b][:, 0:n], ONEB[:], sq, kc == 0, kc == 7, ["ONEB", sk], [pk], inc=(kc == 7))
            rs = RS[:, ti % 2, 0:n]
            rk = "RS%d" % (ti % 2)
            act(rs, PS[b][:, 0:n], AF.Sqrt, [pk, "EPS"], [rk], scale=1.0 / D, bias=EPS_AP[:, 0:1])
            S.op("dve", lambda e: e.reciprocal(out=rs, in_=rs), [rk], [rk])
            return rs, rk

        EPS_T = sb("EPS_T", [128, 2])
        EPS_AP = EPS_T
        S.op("dve", lambda e: e.memset(EPS_T[:], EPS), [], ["EPS"])

        def prenorm(gname, srcdep):
            for ti, (off, n) in enumerate(TTS):
                rs, rk = rstd_for(X, ["X%d" % ti] , ti)
                for kc in range(8):
                    stt(HB[:, kc, off:off + n], X[:, kc, off:off + n], vcol(gname, kc), rs, ALU.mult, ALU.mult,
                        ["X%d" % ti, rk, "VEC", "EPS"], ["HB%d" % ti])

        def postnorm(gname):
            for ti, (off, n) in enumerate(TTS):
                rs, rk = rstd_for(M, ["M0", "M1"], ti)
                for kc in range(8):
                    stt(M[:, kc, off:off + n], M[:, kc, off:off + n], vcol(gname, kc), rs, ALU.mult, ALU.mult,
                        ["M%d" % ti, rk, "VEC", "EPS"], ["M%d" % ti])
                    tt(X[:, kc, off:off + n], X[:, kc, off:off + n], M[:, kc, off:off + n], ALU.add,
                       ["M%d" % ti, "X%d" % ti], ["X%d" % ti])

        HBK = ["HB0", "HB1"]

        def ffn(layer):
            HID = ARENA[:, 0:8448].bitcast(BF16).rearrange("p (a b) -> p a b", a=32)
            R = ARENA[:, 8448:8448 + 528].rearrange("p (a b) -> p a b", a=2)
            for q in range(8):
                W, wk = wload(w_ffn1[layer, q], 8, 512)
                for jj in range(4):
                    j = 4 * q + jj
                    for ti, (off, n) in enumerate(TTS):
                        b = pbank()
                        pk = "P%d" % b
                        for kc in range(8):
                            mm(PS[b][:, 0:n], W[:, kc, jj * 128:(jj + 1) * 128], HB[:, kc, off:off + n], kc == 0, kc == 7,
                               [wk, "HB%d" % ti], [pk], inc=(kc == 7))
                        r = R[:, ti, 0:n]
                        act(r, PS[b][:, 0:n], AF.Relu, [pk], ["R%d" % ti])
                        tt(HID[:, j, off:off + n], r, r, ALU.mult, ["R%d" % ti], ["HID%d_%d" % (j, ti)])
            for c in range(8):
                W, wk = wload(w_ffn2[layer, c], 32, 128)
                for ti, (off, n) in enumerate(TTS):
                    b = pbank()
                    pk = "P%d" % b
                    for j in range(32):
                        mm(PS[b][:, 0:n], W[:, j, :], HID[:, j, off:off + n], j == 0, j == 31,
                           [wk, "HID%d_%d" % (j, ti)], [pk], inc=(j == 31))
                    cp(M[:, c, off:off + n], PS[b][:, 0:n], [pk], ["M%d" % ti], eng="act")

        def run_streams(factories, nslots, extra=None, stag=0):
            pending = list(factories)
            free = list(range(nslots))
            active = []
            extra = list(extra or [])
            rounds = 0
            last_start = -10 ** 9
            while pending or active or extra:
                rounds += 1
                while pending and free and (not active or rounds - last_start >= stag):
                    s_ = free.pop(0)
                    active.append((pending.pop(0)(s_), s_))
                    last_start = rounds
                for item in list(active):
                    gen, s_ = item
                    try:
                        next(gen)
                    except StopIteration:
                        active.remove(item)
                        free.append(s_)
                for gen in list(extra):
                    try:
                        next(gen)
                    except StopIteration:
                        extra.remove(gen)

        def run_all(gen):
            for _ in gen:
                pass

        def conv_block(pc, pck, outf, outk, wname, woff, bname, boff, W, sample_prev, prevk, tab=None):
            tabv = VEC if tab is None else tab
            tabk = "VEC" if tab is None else "VECH"
            w = [tabv[:, VCOLS[wname] + woff + k:VCOLS[wname] + woff + k + 1] for k in range(4)]
            bb = tabv[:, VCOLS[bname] + boff:VCOLS[bname] + boff + 1]
            act(outf[:, 0:W], pc[:, 0:W], AF.Identity, [pck, tabk], [outk], scale=w[0], bias=bb)
            yield
            for k in (1, 2, 3):
                stt(outf[:, 0:W], pc[:, k:k + W], w[k], outf[:, 0:W], ALU.mult, ALU.add, [pck, tabk, outk], [outk])
                yield
            if sample_prev is not None:
                o = outf[:, SEQ:TL]
                act(o, pc[:, 3 + SEQ:3 + TL], AF.Identity, [pck, tabk], [outk], scale=w[3], bias=bb)
                for k in range(3):
                    stt(o, sample_prev[:, :, k], w[k], o, ALU.mult, ALU.add, [prevk, tabk, outk], [outk])
                yield

        def lru(j, part):
            kind = ["seq", "none", "none", "sample"][part]
            W_ = TL if kind == "seq" else SEQ
            Y = ARENA[:, 0:2112].bitcast(BF16).rearrange("p (a b) -> p a b", a=8)
            o = 2112

            def carve(nf32):
                nonlocal o
                v = ARENA[:, o:o + nf32]
                o += nf32
                return v
            NSL = CFG.get("lru_slots", 3)
            T = []
            for s_ in range(NSL):
                d = {}
                d["PC"] = carve(532)[:, 0:531]
                for nm in ("G", "XR", "Rg", "Ig", "Aa", "Tm", "H"):
                    d[nm] = carve(528)
                d["XRB"] = carve(264).bitcast(BF16)
                d["PREV"] = carve(48).rearrange("p (a b) -> p a b", a=NB)
                d["H0S"] = carve(16)
                d["OLC"] = carve(48).rearrange("p (a b) -> p a b", a=NB)
                T.append(d)
            assert o <= ARENA_F32, o
            S.dma("pool", lambda e: e.dma_start(out=WAX[:], in_=w_lru_ax[j]), "WAX", reads=HBK, writes=["WAX"])

            def block(k, s_):
                d = T[s_]
                PC, G, XR, XRB, Rg, Ig, Aa, Tm, H = d["PC"], d["G"], d["XR"], d["XRB"], d["Rg"], d["Ig"], d["Aa"], d["Tm"], d["H"]
                PREV, H0S, OLC = d["PREV"], d["H0S"], d["OLC"]
                K = lambda n: "%s_%d" % (n, s_)
                Wt, wk = wload(w_lru_in[j, k], 8, 256)
                cp(PC[:, 0:3], LCARRY[:, j, k, :], ["LCARRY%d_%d" % (j, k)] + HBK, [K("PC")])
                if kind == "sample":
                    S.dma("sp", lambda e: e.dma_start(out=PREV, in_=s_lconv[j, :, k]), K("PREVL"), reads=HBK, writes=[K("PREVL")])
                    S.dma("sp", lambda e: e.dma_start(out=H0S, in_=s_lh[j, :, k]), K("H0S"), reads=HBK, writes=[K("H0S")])
                yield
                for ti, (off, n) in enumerate(TTS):
                    b = pbank()
                    pk = "P%d" % b
                    for kc in range(8):
                        mm(PS[b][:, 0:n], Wt[:, kc, 0:128], HB[:, kc, off:off + n], kc == 0, kc == 7, [wk, "HB%d" % ti], [pk], inc=(kc == 7))
                    act(G[:, off:off + n], PS[b][:, 0:n], AF.Identity, [pk, "VEC"], [K("G")], bias=vcol("lbin%d" % j, k))
                    yield
                    b = pbank()
                    pk = "P%d" % b
                    for kc in range(8):
                        mm(PS[b][:, 0:n], Wt[:, kc, 128:256], HB[:, kc, off:off + n], kc == 0, kc == 7, [wk, "HB%d" % ti], [pk], inc=(kc == 7))
                    act(PC[:, 3 + off:3 + off + n], PS[b][:, 0:n], AF.Identity, [pk, "VEC"], [K("PC")], bias=vcol("lbin%d" % j, 8 + k))
                    yield
                tt(Tm, G, G, ALU.mult, [K("G")], [K("Tm")])
                yield
                ts(Tm, Tm, 0.044715, 1.0, ALU.mult, ALU.add, [K("Tm")], [K("Tm")])
                yield
                tt(Tm, Tm, G, ALU.mult, [K("Tm"), K("G")], [K("Tm")])
                yield
                act(Tm, Tm, AF.Sigmoid, [K("Tm")], [K("Tm")], scale=1.5957691216)
                yield
                tt(G, G, Tm, ALU.mult, [K("G"), K("Tm")], [K("G")])
                yield
                yield from conv_block(PC, K("PC"), XR, K("XR"), "lcw%d" % j, 4 * k, "lcb%d" % j, k, W_,
                                      PREV if kind == "sample" else None, K("PREVL"))
                cp(LCARRY[:, j, k, :], PC[:, W_:W_ + 3], [K("PC")], ["LCARRY%d_%d" % (j, k)])
                if kind == "sample":
                    cp(OLC[:, :, 0:2], PREV[:, :, 1:3], [K("PREVL")], [K("OLC")])
                    cp(OLC[:, :, 2], PC[:, 3 + SEQ:3 + TL], [K("PC")], [K("OLC")])
                    S.dma("sp", lambda e: e.dma_start(out=o_slconv[j, :, k], in_=OLC), K("OLCD"), reads=[K("OLC")])
                    S.dma("sp", lambda e: e.dma_start(out=o_plconv[j, :, k], in_=PC[:, SEQ:SEQ + 3]), K("OPLC"), reads=[K("PC")])
                cp(XRB, XR, [K("XR")], [K("XRB")], eng="act")
                yield
                for ti, (off, n) in enumerate(TTS):
                    b = pbank()
                    pk = "P%d" % b
                    mm(PS[b][:, 0:n], WAX[:, k, 0:128], XRB[:, off:off + n], True, True, ["WAX", K("XRB")], [pk], inc=True)
                    act(Rg[:, off:off + n], PS[b][:, 0:n], AF.Sigmoid, [pk, "VEC"], [K("Rg")], bias=vcol("lba%d" % j, k))
                    yield
                    b = pbank()
                    pk = "P%d" % b
                    mm(PS[b][:, 0:n], WAX[:, k, 128:256], XRB[:, off:off + n], True, True, ["WAX", K("XRB")], [pk], inc=True)
                    act(Ig[:, off:off + n], PS[b][:, 0:n], AF.Sigmoid, [pk, "VEC"], [K("Ig")], bias=vcol("lbx%d" % j, k))
                    yield
                act(Aa, Rg, AF.Exp, [K("Rg"), "LSC"], [K("Aa")], scale=LSC[:, j, k:k + 1])
                yield
                tt(Tm, Aa, Aa, ALU.mult, [K("Aa")], [K("Tm")])
                yield
                ts(Tm, Tm, -1.0, 1.0, ALU.mult, ALU.add, [K("Tm")], [K("Tm")])
                yield
                act(Tm, Tm, AF.Sqrt, [K("Tm")], [K("Tm")])
                yield
                tt(Tm, Tm, Ig, ALU.mult, [K("Tm"), K("Ig")], [K("Tm")])
                yield
                tt(Tm, Tm, XR, ALU.mult, [K("Tm"), K("XR")], [K("Tm")])
                yield
                S.op("dve", lambda e: e.tensor_tensor_scan(out=H[:, 0:W_], data0=Aa[:, 0:W_], data1=Tm[:, 0:W_],
                                                           initial=HCARRY[:, j, k:k + 1], op0=ALU.mult, op1=ALU.add),
                     [K("Aa"), K("Tm"), "HCARRY%d_%d" % (j, k)], [K("H")])
                yield
                cp(HCARRY[:, j, k:k + 1], H[:, W_ - 1:W_], [K("H")], ["HCARRY%d_%d" % (j, k)])
                if kind == "sample":
                    tt(H[:, SEQ:TL], Aa[:, SEQ:TL], H0S, ALU.mult, [K("Aa"), K("H0S")], [K("H")])
                    tt(H[:, SEQ:TL], H[:, SEQ:TL], Tm[:, SEQ:TL], ALU.add, [K("H"), K("Tm")], [K("H")])
                    S.dma("sp", lambda e: e.dma_start(out=o_slh[j, :, k], in_=H[:, SEQ:TL]), K("OSLH"), reads=[K("H")])
                    S.dma("sp", lambda e: e.dma_start(out=o_plh[j, k], in_=H[:, SEQ - 1:SEQ]), K("OPLH"), reads=[K("H")])
                elif kind == "none":
                    S.op("dve", lambda e: e.memset(H[:, SEQ:TL], 0.0), [K("H")], [K("H")])
                yield
                tt(Y[:, k, :], H, G, ALU.mult, [K("H"), K("G")], ["Y%d" % k])
                yield

            run_streams([(lambda s_, k=k: block(k, s_)) for k in range(8)], NSL, stag=CFG.get("lru_stag", 0))
            for q in range(2):
                Wt, wk = wload(w_lru_out[j, q], 8, 512)
                for cc in range(4):
                    c = 4 * q + cc
                    for ti, (off, n) in enumerate(TTS):
                        b = pbank()
                        pk = "P%d" % b
                        for kc in range(8):
                            mm(PS[b][:, 0:n], Wt[:, kc, cc * 128:(cc + 1) * 128], Y[:, kc, off:off + n], kc == 0, kc == 7,
                               [wk, "Y%d" % kc], [pk], inc=(kc == 7))
                        act(M[:, c, off:off + n], PS[b][:, 0:n], AF.Identity, [pk, "VEC"], ["M%d" % ti], bias=vcol("lbout%d" % j, c))

        def ssd(j, part):
            kind = ["seq", "none", "none", "sample"][part]
            W_ = TL if kind == "seq" else SEQ
            nch = 5 if kind == "seq" else 4
            o = 0

            def carve(nf32):
                nonlocal o
                v = ARENA[:, o:o + nf32]
                o += nf32
                return v
            YF = carve(4224).bitcast(BF16).rearrange("p (a b) -> p a b", a=16)
            XBC2 = [carve(1056).bitcast(BF16).rearrange("p (a b) -> p a b", a=4) for _ in range(3)]
            ZS2 = [carve(528).bitcast(BF16).rearrange("p (a b) -> p a b", a=2) for _ in range(3)]
            PC = carve(532)[:, 0:531]
            CV = carve(528)
            TZ = carve(528)
            DT = carve(528)
            DTM = carve(160).rearrange("p (a b) -> p a b", a=5)
            ATM = carve(160).rearrange("p (a b) -> p a b", a=5)
            ECS = carve(160).rearrange("p (a b) -> p a b", a=5)
            DTE = carve(160).rearrange("p (a b) -> p a b", a=5)
            CD = carve(160).rearrange("p (a b) -> p a b", a=5)
            AMASK = carve(512).rearrange("p (a b) -> p a b", a=NB)
            DECB = carve(512).rearrange("p (a b) -> p a b", a=NB)
            NSL = 2
            TS_ = []
            for s_ in range(NSL):
                d = {}
                for nm in ("XTM", "XDT", "XDTE", "ZTM", "Y3"):
                    d[nm] = carve(128).bitcast(BF16)
                d["BTM"] = carve(64).bitcast(BF16)
                d["CBM"] = carve(64).bitcast(BF16)
                d["AL"] = carve(512).rearrange("p (a b) -> p a b", a=4)
                d["ED"] = carve(256).bitcast(BF16).rearrange("p (a b) -> p a b", a=4)
                d["SC"] = carve(256).bitcast(BF16).rearrange("p (a b) -> p a b", a=4)
                for nm in ("Y1", "Y2", "JK"):
                    d[nm] = carve(256)
                d["SS"] = carve(2)
                TS_.append(d)
            CMASK = carve(256).rearrange("p (a b) -> p a b", a=NB)
            BMASK = carve(1024).bitcast(BF16).rearrange("p (a b) -> p a b", a=NB)
            SIN = carve(1024).rearrange("p (a b) -> p a b", a=4)
            SNEW = carve(1024).rearrange("p (a b) -> p a b", a=4)
            PREV = carve(48).rearrange("p (a b) -> p a b", a=NB)
            OSC = carve(48).rearrange("p (a b) -> p a b", a=NB)
            assert o <= ARENA_F32, o
            hk = "HT%d" % j
            hkg = lambda g: "HT%d_%d" % (j, g)
            hbg = lambda g: "HTB_%d" % g
            allhk = [hkg(g) for g in range(8)]
            aneg = ANEG[:, j, :]

            S.dma("pool", lambda e: e.dma_start(out=WDT[:], in_=w_ssd_dt[j]), "WDT", reads=HBK, writes=["WDT"])
            cp(HTB[:], HT[:, j, :], allhk + HBK, [hbg(g) for g in range(8)], eng="act")
            for ti, (off, n) in enumerate(TTS):
                b = pbank()
                pk = "P%d" % b
                for kc in range(8):
                    mm(PS[b][0:32, 0:n], WDT[:, kc, :], HB[:, kc, off:off + n], kc == 0, kc == 7, ["WDT", "HB%d" % ti], [pk], inc=(kc == 7))
                act(DT[0:32, off:off + n], PS[b][0:32, 0:n], AF.Exp, [pk, "VEC"], ["DT"], bias=VEC[0:32, VCOLS["sdt%d" % j]:VCOLS["sdt%d" % j] + 1])
            act(DT[0:32, :], DT[0:32, :], AF.Ln, ["DT"], ["DT"], bias=1.0)
            chunks = [(c * 128, 128) for c in range(4)] + [(SEQ, NTAIL)]
            ncht = 5 if kind != "none" else 4
            for ch in range(ncht):
                c0, Q = chunks[ch]
                b = pbank()
                pk = "P%d" % b
                tr(PS[b][0:Q, 0:32], DT[0:32, c0:c0 + Q], IDF[0:32, 0:32], ["DT", "CST"], [pk])
                cp(DTM[0:Q, ch, :], PS[b][0:Q, 0:32], [pk], ["DTM"])
                tt(ATM[0:Q, ch, :], DTM[0:Q, ch, :], aneg[0:Q, :], ALU.mult, ["DTM", "ANEG"], ["ATM"])
            for ch in range(nch):
                c0, Q = chunks[ch]
                for (dst, lhs, dk) in ((ECS, VM, "ECS"), (DTE, MLOW, "DTE")):
                    b = pbank()
                    pk = "P%d" % b
                    mm(PS[b][0:Q, 0:32], lhs[0:Q, 0:Q], ATM[0:Q, ch, :], True, True, ["CST", "ATM"], [pk], inc=True)
                    act(dst[0:Q, ch, :], PS[b][0:Q, 0:32], AF.Exp, [pk], [dk])
                b = pbank()
                pk = "P%d" % b
                mm(PS[b][:, 0:32], ONEF[0:Q, :], ATM[0:Q, ch, :], True, True, ["CST", "ATM"], [pk], inc=True)
                act(CD[:, ch, :], PS[b][:, 0:32], AF.Exp, [pk], ["CD"])
            if kind == "sample":
                tt(AMASK[0:NB], ATM[0:NB, 4, :].unsqueeze(1).to_broadcast([NB, NB, 32]),
                   IDF[0:NB, 0:NB].unsqueeze(2).to_broadcast([NB, NB, 32]), ALU.mult, ["ATM", "CST"], ["AMASK"])
                b = pbank()
                pk = "P%d" % b
                mm(PS[b][:, 0:512], ONEF[0:NB, :], AMASK[0:NB].rearrange("p a b -> p (a b)"), True, True, ["CST", "AMASK"], [pk], inc=True)
                act(DECB.rearrange("p a b -> p (a b)"), PS[b][:, 0:512], AF.Exp, [pk], ["DECB"])

            def gen_inproj(g):
                XBC, ZS = XBC2[g % 3], ZS2[g % 3]
                xk, zk = "XBC%d" % (g % 3), "ZS%d" % (g % 3)
                Wa, wka = wload(w_ssd_in[j, g, 0], 8, 384)
                Wb, wkb = wload(w_ssd_in[j, g, 1], 8, 384)
                blocks = [(Wa, wka, 0), (Wa, wka, 128), (Wa, wka, 256), (Wb, wkb, 0), (Wb, wkb, 128), (Wb, wkb, 256)]
                ccs = [2 * g, 2 * g + 1, 16 + g, 24 + g]
                for fb in range(6):
                    Wt, wk, wo = blocks[fb]
                    if fb >= 2:
                        ci = fb - 2
                        cc = ccs[ci]
                        cp(PC[:, 0:3], SCARRY[:, j, cc, :], ["SCARRY%d_%d" % (j, cc)] + HBK, ["PC"])
                        if kind == "sample":
                            S.dma("sp", lambda e, cc=cc: e.dma_start(out=PREV, in_=s_sconv[j, :, cc]), "PREVS", reads=HBK, writes=["PREVS"])
                    for ti, (off, n) in enumerate(TTS):
                        b = pbank()
                        pk = "P%d" % b
                        for kc in range(8):
                            mm(PS[b][:, 0:n], Wt[:, kc, wo:wo + 128], HB[:, kc, off:off + n], kc == 0, kc == 7, [wk, "HB%d" % ti], [pk], inc=(kc == 7))
                        if fb < 2:
                            act(TZ[:, off:off + n], PS[b][:, 0:n], AF.Tanh, [pk], ["TZ"], scale=0.5)
                            stt(ZS[:, fb, off:off + n], TZ[:, off:off + n], 1.0, PS[b][:, 0:n], ALU.add, ALU.mult, ["TZ", pk], [zk])
                        else:
                            cp(PC[:, 3 + off:3 + off + n], PS[b][:, 0:n], [pk], ["PC"], eng="act")
                        yield
                    if fb >= 2:
                        yield from conv_block(PC, "PC", CV, "CV", "scw%d" % j, 4 * cc, "scb%d" % j, cc, W_,
                                              PREV if kind == "sample" else None, "PREVS", tab=VECH)
                        cp(SCARRY[:, j, cc, :], PC[:, W_:W_ + 3], ["PC"], ["SCARRY%d_%d" % (j, cc)])
                        if kind == "sample":
                            cp(OSC[:, :, 0:2], PREV[:, :, 1:3], ["PREVS"], ["OSC"])
                            cp(OSC[:, :, 2], PC[:, 3 + SEQ:3 + TL], ["PC"], ["OSC"])
                            S.dma("sp", lambda e, cc=cc: e.dma_start(out=o_ssconv[j, :, cc], in_=OSC), "OSCD", reads=["OSC"])
                            S.dma("sp", lambda e, cc=cc: e.dma_start(out=o_psconv[j, :, cc], in_=PC[:, SEQ:SEQ + 3]), "OPSC", reads=["PC"])
                        if kind == "none":
                            S.op("dve", lambda e: e.memset(CV[:, SEQ:TL], 0.0), ["CV"], ["CV"])
                        act(TZ[:, :], CV, AF.Tanh, ["CV"], ["TZ"])
                        yield
                        stt(XBC[:, ci, :], TZ[:, :], 1.0, CV, ALU.add, ALU.mult, ["TZ", "CV"], [xk])
                        yield

            state_ready = {}

            def gen_chunk(g, ch, s_):
                d = TS_[s_]
                XTM, XDT, XDTE, ZTM, BTM, CBM, AL, ED, SC, Y1, Y2, JK, Y3, SS = (d[n] for n in
                    ("XTM", "XDT", "XDTE", "ZTM", "BTM", "CBM", "AL", "ED", "SC", "Y1", "Y2", "JK", "Y3", "SS"))
                K = lambda n: "%s_%d" % (n, s_)
                XBC, ZS = XBC2[g % 3], ZS2[g % 3]
                xk, zk = "XBC%d" % (g % 3), "ZS%d" % (g % 3)
                gs = slice(4 * g, 4 * g + 4)
                c0, Q = chunks[ch]
                decode = (ch == 4 and kind == "sample")
                cols = slice(c0, c0 + Q)
                r4 = lambda ap: ap.rearrange("p (a b) -> p a b", a=4)
                pk1 = "PBf"
                pv = PB[:, 0:640]
                for i in range(2):
                    tr(pv[0:Q, i * 128:(i + 1) * 128], XBC[:, i, cols], IDB[:], [xk, "IDB"], [pk1], inc=False)
                for i in range(2):
                    tr(pv[0:Q, 256 + i * 128:256 + (i + 1) * 128], ZS[:, i, cols], IDB[:], [zk, "IDB"], [pk1], inc=False)
                tr(pv[0:Q, 512:640], XBC[:, 2, cols], IDB[:], [xk, "IDB"], [pk1], inc=True)
                cp(XTM[0:Q, :], pv[0:Q, 0:256], [pk1], [K("XTM")], eng="act")
                act(ZTM[0:Q, :], pv[0:Q, 256:512], AF.Copy, [pk1], [K("ZTM")], scale=0.5)
                cp(BTM[0:Q, :], pv[0:Q, 512:640], [pk1], [K("BTM")], eng="act")
                yield
                tt(r4(XDT[0:Q, :]), r4(XTM[0:Q, :]), DTM[0:Q, ch, gs].unsqueeze(2).to_broadcast([Q, 4, 64]), ALU.mult, [K("XTM"), "DTM"], [K("XDT")])
                yield
                if not decode:
                    tt(r4(XDTE[0:Q, :]), r4(XDT[0:Q, :]), DTE[0:Q, ch, gs].unsqueeze(2).to_broadcast([Q, 4, 64]), ALU.mult, [K("XDT"), "DTE"], [K("XDTE")])
                    yield
                    while state_ready.get(g, 0) != ch:
                        yield
                    bO = pbank()
                    pkO = "P%d" % bO
                    mm(PS[bO][0:Q, 0:256], XBC[:, 3, cols], HTB[:, g * 256:(g + 1) * 256], True, True, [xk, hbg(g)], [pkO], inc=True)
                    tt(r4(Y1[0:Q, :]), r4(PS[bO][0:Q, 0:256]), ECS[0:Q, ch, gs].unsqueeze(2).to_broadcast([Q, 4, 64]), ALU.mult, [pkO, "ECS"], [K("Y1")])
                    yield
                    htg = HT[:, j, g * 256:(g + 1) * 256]
                    tt(r4(htg), r4(htg), CD[:, ch, gs].unsqueeze(2).to_broadcast([128, 4, 64]), ALU.mult, [hkg(g), "CD"], [hkg(g)])
                    b4 = pbank()
                    pk4 = "P%d" % b4
                    mm(PS[b4][:, 0:256], BTM[0:Q, :], XDTE[0:Q, :], True, True, [K("BTM"), K("XDTE")], [pk4], inc=True)
                    tt(htg, htg, PS[b4][:, 0:256], ALU.add, [hkg(g), pk4], [hkg(g)])
                    cp(HTB[:, g * 256:(g + 1) * 256], htg, [hkg(g)], [hbg(g)], eng="act")
                    state_ready[g] = ch + 1
                    yield
                    b2 = pbank()
                    pk2 = "P%d" % b2
                    mm(PS[b2][0:Q, 0:Q], XBC[:, 2, cols], XBC[:, 3, cols], True, True, [xk], [pk2], inc=True)
                    tt(CBM[0:Q, 0:Q], PS[b2][0:Q, 0:Q], VM[0:Q, 0:Q], ALU.mult, [pk2, "CST"], [K("CBM")])
                    yield
                    tt(AL[0:Q, :, 0:Q], MLOW[0:Q, 0:Q].unsqueeze(1).to_broadcast([Q, 4, Q]),
                       ATM[0:Q, ch, gs].unsqueeze(2).to_broadcast([Q, 4, Q]), ALU.mult, ["CST", "ATM"], [K("AL")])
                    yield
                    b3 = pbank()
                    pk3 = "P%d" % b3
                    for h4 in range(4):
                        mm(PS[b3][0:Q, h4 * 128:h4 * 128 + Q], AL[0:Q, h4, 0:Q], VM[0:Q, 0:Q], True, True, [K("AL"), "CST"], [pk3], inc=(h4 == 3))
                    act(ED[0:Q, :, 0:Q], PS[b3][0:Q, :].rearrange("p (a b) -> p a b", a=4)[:, :, 0:Q], AF.Exp, [pk3], [K("ED")])
                    yield
                    tt(SC[0:Q, :, 0:Q], ED[0:Q, :, 0:Q], CBM[0:Q, 0:Q].unsqueeze(1).to_broadcast([Q, 4, Q]), ALU.mult, [K("ED"), K("CBM")], [K("SC")])
                    yield
                    XD = JK.bitcast(BF16)[:, 0:256]
                    tt(r4(XD[0:Q, :]), r4(XTM[0:Q, :]), dskb(j, g)[0:Q, :].unsqueeze(2).to_broadcast([Q, 4, 64]), ALU.mult, [K("XTM"), "ALOGB"], [K("JK")])
                    yield
                    bY = pbank()
                    pkY = "P%d" % bY
                    mm(PS[bY][0:Q, 0:256], IDB[0:Q, 0:Q], XD[0:Q, :], True, False, ["IDB", K("JK")], [pkY], inc=False)
                    for h4 in range(4):
                        mm(PS[bY][0:Q, h4 * 64:(h4 + 1) * 64], SC[0:Q, h4, 0:Q], XDT[0:Q, h4 * 64:(h4 + 1) * 64], False, h4 == 3,
                           [K("SC"), K("XDT")], [pkY], inc=(h4 == 3))
                    tt(Y1[0:Q, :], Y1[0:Q, :], PS[bY][0:Q, 0:256], ALU.add, [K("Y1"), pkY], [K("Y1")])
                    yield
                else:
                    bY = pbank()
                    pkY = "P%d" % bY
                    tt(CMASK[:, :, :], XBC[:, 3, cols].unsqueeze(2).to_broadcast([128, NB, NB]),
                       I16B.rearrange("p a b -> p (a b)").rearrange("p (a b) -> p a b", a=NB), ALU.mult, [xk, "CST"], ["CMASK"])
                    tt(BMASK[0:NB], BTM[0:NB, :].unsqueeze(1).to_broadcast([NB, NB, 128]),
                       IDF[0:NB, 0:NB].unsqueeze(2).to_broadcast([NB, NB, 128]), ALU.mult, [K("BTM"), "CST"], ["BMASK"])
                    for bi in range(NB):
                        sl = bi % 4
                        sk, nk = "SIN%d" % sl, "SNEW%d" % sl
                        S.dma("sp", lambda e, bi=bi, sl=sl: e.dma_start(out=SIN[:, sl, :], in_=s_sh[j, bi, :, g * 256:(g + 1) * 256]),
                              sk, reads=HBK, writes=[sk])
                        tt(r4(SNEW[:, sl, :]), r4(SIN[:, sl, :]), DECB[:, bi, gs].unsqueeze(2).to_broadcast([128, 4, 64]), ALU.mult, [sk, "DECB"], [nk])
                        b5 = pbank()
                        if b5 == bY:
                            b5 = pbank()
                        pk5 = "P%d" % b5
                        mm(PS[b5][:, 0:256], BMASK[0:NB, bi, :], XDT[0:NB, :], True, True, ["BMASK", K("XDT")], [pk5], inc=True)
                        tt(SNEW[:, sl, :], SNEW[:, sl, :], PS[b5][:, 0:256], ALU.add, [nk, pk5], [nk])
                        S.dma("pool", lambda e, bi=bi, sl=sl: e.dma_start(out=o_ssh[j, bi, :, g * 256:(g + 1) * 256], in_=SNEW[:, sl, :]),
                              "O" + nk, reads=[nk])
                        mm(PS[bY][0:NB, 0:256], CMASK[:, bi, :], SNEW[:, sl, :], bi == 0, bi == NB - 1, ["CMASK", nk], [pkY], inc=True)
                    cp(Y1[0:Q, :], PS[bY][0:Q, 0:256], [pkY], [K("Y1")])
                    yield
                    tt(r4(JK[0:Q, :]), r4(XTM[0:Q, :]), dskb(j, g)[0:Q, :].unsqueeze(2).to_broadcast([Q, 4, 64]), ALU.mult, [K("XTM"), "ALOGB"], [K("JK")])
                    yield
                    tt(Y1[0:Q, :], Y1[0:Q, :], JK[0:Q, :], ALU.add, [K("Y1"), K("JK")], [K("Y1")])
                    yield
                tt(Y2[0:Q, :], Y1[0:Q, :], ZTM[0:Q, :], ALU.mult, [K("Y1"), K("ZTM")], [K("Y2")])
                yield
                tt(JK[0:Q, :], Y2[0:Q, :], Y2[0:Q, :], ALU.mult, [K("Y2"), K("JK")], [K("JK")])
                yield
                S.op("dve", lambda e: e.tensor_reduce(out=SS[0:Q, 0:1], in_=JK[0:Q, :], axis=AX.X, op=ALU.add), [K("JK")], [K("SS")])
                yield
                ts(SS[0:Q, 0:1], SS[0:Q, 0:1], 1.0 / 256, EPS, ALU.mult, ALU.add, [K("SS")], [K("SS")], eng="pool")
                tt(SS[0:Q, 0:1], SS[0:Q, 0:1], NHALF[0:Q, 0:1], ALU.pow, [K("SS"), "NHALF"], [K("SS")], eng="pool")
                yield
                act(Y3[0:Q, :], Y2[0:Q, :], AF.Identity, [K("Y2"), K("SS")], [K("Y3")], scale=SS[0:Q, 0:1])
                yield
                pk6 = "PBb"
                pv6 = PB2[:, 0:256]
                for i in range(2):
                    tr(pv6[:, i * 128:i * 128 + Q], Y3[0:Q, i * 128:(i + 1) * 128], IDB[0:Q, 0:Q], [K("Y3"), "IDB"], [pk6], inc=(i == 1))
                for i in range(2):
                    act(YF[:, 2 * g + i, cols], pv6[:, i * 128:i * 128 + Q], AF.Identity, [pk6, "VEC"], ["YF%d" % (2 * g + i)],
                        scale=vcol("snorm%d" % j, 2 * g + i))
                yield

            run_all(gen_inproj(0))
            inproj_done = {0: True}
            spawned = {0}

            def wrap_inproj(g):
                yield from gen_inproj(g)
                inproj_done[g] = True

            pending = [(g, ch) for g in range(8) for ch in range(ncht)]
            active = []
            free = list(range(NSL))
            extra = []
            xsteps = CFG.get("extra_steps", 1)
            while pending or active or extra:
                while pending and free and inproj_done.get(pending[0][0]):
                    g, ch = pending.pop(0)
                    s_ = free.pop(0)
                    active.append((gen_chunk(g, ch, s_), s_))
                    if g + 1 < 8 and (g + 1) not in spawned:
                        spawned.add(g + 1)
                        extra.append(wrap_inproj(g + 1))
                for item in list(active):
                    try:
                        next(item[0])
                    except StopIteration:
                        active.remove(item)
                        free.append(item[1])
                for gen in list(extra):
                    try:
                        for _ in range(xsteps):
                            next(gen)
                    except StopIteration:
                        extra.remove(gen)
            if kind == "none":
                for c16 in range(16):
                    S.op("dve", lambda e, c16=c16: e.memset(YF[:, c16, SEQ:TL], 0.0), ["YF%d" % c16], ["YF%d" % c16])
            if part == NPART - 1:
                S.dma("sp", lambda e: e.dma_start(out=o_psh[j], in_=HT[:, j, :]), "OPSH%d" % j, reads=allhk)
            for q in range(4):
                Wt, wk = wload(w_ssd_out[j, q], 16, 256)
                for cc2 in range(2):
                    c = 2 * q + cc2
                    for ti, (off, n) in enumerate(TTS):
                        b = pbank()
                        pk = "P%d" % b
                        for kc in range(16):
                            mm(PS[b][:, 0:n], Wt[:, kc, cc2 * 128:(cc2 + 1) * 128], YF[:, kc, off:off + n], kc == 0, kc == 15,
                               [wk, "YF%d" % kc], [pk], inc=(kc == 15))
                        cp(M[:, c, off:off + n], PS[b][:, 0:n], [pk], ["M%d" % ti], eng="act")

        XK = ["X0", "X1"]
        for part in CFG["parts"]:
            S.dma("sp", lambda e, part=part: e.dma_start(out=X[:], in_=xin[:, :, part, :]), "XLD", writes=XK)
            for layer in CFG["layers"]:
                j = layer // 2
                if CFG["mixer"]:
                    prenorm("mpre%d" % layer, None)
                    if layer % 2 == 0:
                        ssd(j, part)
                    else:
                        lru(j, part)
                    postnorm("mpost%d" % layer)
                if CFG["ffn"]:
                    prenorm("fpre%d" % layer, None)
                    ffn(layer)
                    postnorm("fpost%d" % layer)
            S.dma("sp", lambda e, part=part: e.dma_start(out=yout[:, :, part, :], in_=X[:]), "XST", reads=XK)
        S.finish("sp")
        S.emit()
        build_nc.stats = dict(S.count), {k: len(v) for k, v in S.ops.items()}, dict(S.dcount)
    return nc


_NC_CACHE = {}


def _consts():
    c = np.zeros((128, 6, 128), np.float32)
    c[:, 0] = np.eye(128)
    t = np.arange(128)
    c[:, 1] = (t[:, None] <= t[None, :])
    c[:, 2] = (t[:, None] > t[None, :])
    c[:, 3] = 1.0
    i16 = np.eye(16, dtype=np.float32).reshape(-1)
    c[:, 4:6] = np.broadcast_to(i16.reshape(1, 2, 128), (128, 2, 128))
    return c


def kernel(x_prompt, x_sample, state_ssd_conv, state_ssd_h, state_lru_conv, state_lru_h, meta_tokens,
           norm_mix_pre, norm_mix_post, norm_ffn_pre, norm_ffn_post,
           ssd_w_in, ssd_conv_w, ssd_conv_b, ssd_dt_bias, ssd_a_log, ssd_d, ssd_norm, ssd_w_out,
           lru_w_in, lru_b_in, lru_conv_w, lru_conv_b, lru_w_a, lru_b_a, lru_w_x, lru_b_x, lru_lambda,
           lru_w_out, lru_b_out, ffn_w1, ffn_w2):
    f = lambda a: np.asarray(a, dtype=np.float32)
    x_prompt, x_sample = f(x_prompt), f(x_sample)
    NCORE = 8
    vm = VecMap()
    norms = {"mpre": f(norm_mix_pre), "mpost": f(norm_mix_post), "fpre": f(norm_ffn_pre), "fpost": f(norm_ffn_post)}
    for i in range(4):
        for kind in ("mpre", "mpost", "fpre", "fpost"):
            vm.add("%s%d" % (kind, i), fm(norms[kind][i]))
    for j in range(2):
        cw = f(ssd_conv_w)[j]
        vm.add("scw%d" % j, cw.reshape(4, 32, 128).transpose(2, 1, 0).reshape(128, 128))
        vm.add("scb%d" % j, fm(f(ssd_conv_b)[j]))
        vm.add("snorm%d" % j, fm(f(ssd_norm)[j]))
        dtb = np.zeros((128, 1), np.float32)
        dtb[0:32, 0] = f(ssd_dt_bias)[j]
        vm.add("sdt%d" % j, dtb)
        vm.add("lbin%d" % j, fm(f(lru_b_in)[j]))
        lw = f(lru_conv_w)[j]
        vm.add("lcw%d" % j, lw.reshape(4, 8, 128).transpose(2, 1, 0).reshape(128, 32))
        vm.add("lcb%d" % j, fm(f(lru_conv_b)[j]))
        vm.add("lba%d" % j, fm(f(lru_b_a)[j]))
        vm.add("lbx%d" % j, fm(f(lru_b_x)[j]))
        vm.add("llam%d" % j, fm(f(lru_lambda)[j]))
        vm.add("lbout%d" % j, fm(f(lru_b_out)[j]))
    assert vm.cols == VCOLS and vm.n == NV
    vec = vm.table()
    vecb = np.zeros((1, 128), np.float32)
    for j in range(2):
        vecb[0, j * 64:j * 64 + 32] = f(ssd_a_log)[j]
        vecb[0, j * 64 + 32:j * 64 + 64] = f(ssd_d)[j]

    def kp(w):
        K, N = w.shape
        return np.ascontiguousarray(w.reshape(K // 128, 128, N).transpose(1, 0, 2))

    DI = 2048
    w_ssd_in = np.zeros((2, 8, 2, 128, 8, 384), np.float32)
    w_ssd_dt = np.zeros((2, 128, 8, 32), np.float32)
    w_ssd_out = np.zeros((2, 4, 128, 16, 256), np.float32)
    for j in range(2):
        w = f(ssd_w_in)[j]
        for g in range(8):
            cols = np.concatenate([
                np.arange(256 * g, 256 * g + 256),
                DI + np.arange(256 * g, 256 * g + 256),
                DI + 2048 + np.arange(128 * g, 128 * g + 128),
                DI + 3072 + np.arange(128 * g, 128 * g + 128)])
            blk = kp(w[:, cols])
            w_ssd_in[j, g, 0] = blk[:, :, 0:384]
            w_ssd_in[j, g, 1] = blk[:, :, 384:768]
        w_ssd_dt[j] = kp(w[:, DI + 4096:DI + 4096 + 32])
        wo = kp(f(ssd_w_out)[j])
        for q in range(4):
            w_ssd_out[j, q] = wo[:, :, q * 256:(q + 1) * 256]
    w_lru_in = np.zeros((2, 8, 128, 8, 256), np.float32)
    w_lru_ax = np.zeros((2, 128, 8, 256), np.float32)
    w_lru_out = np.zeros((2, 2, 128, 8, 512), np.float32)
    for j in range(2):
        w = kp(f(lru_w_in)[j])
        for k in range(8):
            w_lru_in[j, k, :, :, 0:128] = w[:, :, k * 128:(k + 1) * 128]
            w_lru_in[j, k, :, :, 128:256] = w[:, :, 1024 + k * 128:1024 + (k + 1) * 128]
            w_lru_ax[j, :, k, 0:128] = f(lru_w_a)[j, k]
            w_lru_ax[j, :, k, 128:256] = f(lru_w_x)[j, k]
        wo = kp(f(lru_w_out)[j])
        for q in range(2):
            w_lru_out[j, q] = wo[:, :, q * 512:(q + 1) * 512]
    w_ffn1 = np.zeros((4, 8, 128, 8, 512), np.float32)
    w_ffn2 = np.zeros((4, 8, 128, 32, 128), np.float32)
    for i in range(4):
        w1 = kp(f(ffn_w1)[i])
        w2 = kp(f(ffn_w2)[i])
        for q in range(8):
            w_ffn1[i, q] = w1[:, :, q * 512:(q + 1) * 512]
            w_ffn2[i, q] = w2[:, :, q * 128:(q + 1) * 128]
    cst = _consts()

    in_maps = []
    meta = f(meta_tokens)
    ssc, ssh, slc, slh = f(state_ssd_conv), f(state_ssd_h), f(state_lru_conv), f(state_lru_h)
    for c in range(NCORE):
        seq = np.concatenate([meta, x_prompt[c]], axis=0)
        cols = np.zeros((NPART, TL, D), np.float32)
        cols[0] = seq[0:528]
        cols[1, 0:512] = seq[528:1040]
        cols[2, 0:512] = seq[1040:1552]
        cols[3, 0:512] = seq[1552:2064]
        cols[3, 512:528] = x_sample[c * NB:(c + 1) * NB, 0]
        xin = np.ascontiguousarray(cols.reshape(NPART, TL, 8, 128).transpose(3, 2, 0, 1))
        bs = slice(c * NB, (c + 1) * NB)
        s_sconv = np.ascontiguousarray(ssc[:, bs].reshape(2, NB, 3, 32, 128).transpose(0, 4, 3, 1, 2))
        s_sh = np.ascontiguousarray(ssh[:, bs].reshape(2, NB, 2048, 128).transpose(0, 1, 3, 2))
        s_lconv = np.ascontiguousarray(slc[:, bs].reshape(2, NB, 3, 8, 128).transpose(0, 4, 3, 1, 2))
        s_lh = np.ascontiguousarray(slh[:, bs].reshape(2, NB, 8, 128).transpose(0, 3, 2, 1))
        in_maps.append(dict(xin=xin, cst=cst, vec=vec, vecb=vecb, w_ssd_in=w_ssd_in, w_ssd_dt=w_ssd_dt, w_ssd_out=w_ssd_out,
                            w_lru_in=w_lru_in, w_lru_ax=w_lru_ax, w_lru_out=w_lru_out, w_ffn1=w_ffn1, w_ffn2=w_ffn2,
                            s_sconv=s_sconv, s_sh=s_sh, s_lconv=s_lconv, s_lh=s_lh))
    if CFG["ncore"] < NCORE:
        in_maps = in_maps[:CFG["ncore"]]
    if "nc" not in _NC_CACHE:
        _NC_CACHE["nc"] = build_nc()
    nc = _NC_CACHE["nc"]
    if CFG.get("trace"):
        res = run_bass_kernel_spmd(nc, in_maps, core_ids=list(range(len(in_maps))), trace=True)
        print("EXEC_TIME_NS", res.exec_time_ns)
    else:
        res = run_bass_kernel_spmd(nc, in_maps, core_ids=list(range(len(in_maps))))
    R = list(res.results)
    while len(R) < NCORE:
        R.append(R[0])

    y_prompt = np.zeros((8, 2048, D), np.float32)
    y_sample = np.zeros((128, 1, D), np.float32)
    p_ssd_conv = np.zeros((2, 8, 3, 4096), np.float32)
    p_ssd_h = np.zeros((2, 8, 32, 64, 128), np.float32)
    p_lru_conv = np.zeros((2, 8, 3, 1024), np.float32)
    p_lru_h = np.zeros((2, 8, 1024), np.float32)
    s_ssd_conv = np.zeros((2, 128, 3, 4096), np.float32)
    s_ssd_h = np.zeros((2, 128, 32, 64, 128), np.float32)
    s_lru_conv = np.zeros((2, 128, 3, 1024), np.float32)
    s_lru_h = np.zeros((2, 128, 1024), np.float32)
    for c in range(NCORE):
        r = R[c]
        yo = np.asarray(r["yout"]).transpose(2, 3, 1, 0).reshape(NPART, TL, D)
        seq = np.concatenate([yo[0, 0:528], yo[1, 0:512], yo[2, 0:512], yo[3, 0:512]], axis=0)
        y_prompt[c] = seq[16:]
        bs = slice(c * NB, (c + 1) * NB)
        y_sample[bs, 0] = yo[3, 512:528]
        p_ssd_conv[:, c] = np.asarray(r["o_psconv"]).transpose(0, 3, 2, 1).reshape(2, 3, 4096)
        s_ssd_conv[:, bs] = np.asarray(r["o_ssconv"]).transpose(0, 3, 4, 2, 1).reshape(2, NB, 3, 4096)
        p_ssd_h[:, c] = np.asarray(r["o_psh"]).transpose(0, 2, 1).reshape(2, 32, 64, 128)
        s_ssd_h[:, bs] = np.asarray(r["o_ssh"]).transpose(0, 1, 3, 2).reshape(2, NB, 32, 64, 128)
        p_lru_conv[:, c] = np.asarray(r["o_plconv"]).transpose(0, 3, 2, 1).reshape(2, 3, 1024)
        s_lru_conv[:, bs] = np.asarray(r["o_slconv"]).transpose(0, 3, 4, 2, 1).reshape(2, NB, 3, 1024)
        p_lru_h[:, c] = np.asarray(r["o_plh"]).reshape(2, 1024)
        s_lru_h[:, bs] = np.asarray(r["o_slh"]).transpose(0, 3, 2, 1).reshape(2, NB, 1024)
    return (y_prompt, y_sample, p_ssd_conv, p_ssd_h, p_lru_conv, p_lru_h,
            s_ssd_conv, s_ssd_h, s_lru_conv, s_lru_h)
```

```python
import numpy as np
from contextlib import ExitStack
import concourse.bass as bass
import concourse.mybir as mybir
from concourse.bass_utils import run_bass_kernel_spmd

F32 = mybir.dt.float32
BF16 = mybir.dt.bfloat16
AF = mybir.ActivationFunctionType
ALU = mybir.AluOpType
AX = mybir.AxisListType
ENGS = ["pe", "act", "dve", "pool", "sp"]

D = 1024
NPART = 4
TL = 528
SEQ = 512
NTAIL = 16
TTS = [(0, 264), (264, 264)]
EPS = 1e-6
NB = 16


class Sched:
    def __init__(self, nc):
        self.nc = nc
        self.ops = {e: [] for e in ENGS}
        self.count = {e: 0 for e in ENGS}
        self.sem = {e: nc.alloc_semaphore(name="s_" + e) for e in ENGS}
        self.seen = {e: {} for e in ENGS}
        self.last_w = {}
        self.readers = {}
        self.dsem = {}
        self.dcount = {}
        self.pending_pe = False

    def _semh(self, key):
        return self.sem[key] if key in self.sem else self.dsem[key]

    def _deps(self, eng, reads, writes):
        deps = []
        for b in reads:
            if b in self.last_w:
                deps.append(self.last_w[b])
        for b in writes:
            if b in self.last_w:
                deps.append(self.last_w[b])
            deps.extend(self.readers.get(b, ()))
        waits = {}
        for (k, v) in deps:
            if k == eng and eng == "pe":
                continue
            if self.seen[eng].get(k, 0) >= v:
                continue
            if waits.get(k, 0) < v:
                waits[k] = v
        for k, v in waits.items():
            self.seen[eng][k] = v
        return waits

    def _record(self, ev, reads, writes):
        for b in reads:
            self.readers.setdefault(b, []).append(ev)
        for b in writes:
            self.last_w[b] = ev
            self.readers[b] = []

    def op(self, eng, fn, reads=(), writes=(), inc=True):
        waits = self._deps(eng, reads, writes)
        ev = (eng, self.count[eng] + 1)
        if inc:
            self.count[eng] += 1
            if eng == "pe":
                self.pending_pe = False
        else:
            self.pending_pe = True
        self.ops[eng].append((waits, fn, eng if inc else None, 1))
        self._record(ev, reads, writes)

    def dma(self, queue, fn, key, reads=(), writes=()):
        if key not in self.dsem:
            self.dsem[key] = self.nc.alloc_semaphore(name="d_" + str(key))
            self.dcount[key] = 0
        waits = self._deps(queue, reads, writes)
        self.dcount[key] += 16
        ev = (key, self.dcount[key])
        self.ops[queue].append((waits, fn, key, 16))
        self._record(ev, reads, writes)

    def finish(self, queue="sp"):
        waits = dict(self.dcount)
        for e in ENGS:
            if e != queue and self.count[e] > 0:
                waits[e] = self.count[e]
        self.ops[queue].append((waits, None, None, 0))

    def emit(self):
        nc = self.nc
        assert not self.pending_pe
        engobj = {"pe": "tensor", "act": "scalar", "dve": "vector", "pool": "gpsimd", "sp": "sync"}
        with nc.Block() as block:
            for e in ENGS:
                ops = self.ops[e]

                def body(eo, ops=ops):
                    for (waits, fn, inc_key, amt) in ops:
                        for k, v in waits.items():
                            eo.wait_ge(self._semh(k), v)
                        if fn is None:
                            continue
                        ins = fn(eo)
                        if inc_key is not None:
                            ins.then_inc(self._semh(inc_key), amt)

                getattr(block, engobj[e])(body)


class VecMap:
    def __init__(self):
        self.cols = {}
        self.data = []
        self.n = 0

    def add(self, name, arr):
        arr = np.ascontiguousarray(arr, dtype=np.float32).reshape(128, -1)
        self.cols[name] = self.n
        self.data.append(arr)
        self.n += arr.shape[1]

    def table(self):
        return np.concatenate(self.data, axis=1)


def fm(v):
    return np.ascontiguousarray(np.asarray(v, dtype=np.float32).reshape(-1, 128).T)


def build_vec_layout():
    cols = {}
    n = 0

    def add(name, k):
        nonlocal n
        cols[name] = n
        n += k
    for i in range(4):
        for kind in ("mpre", "mpost", "fpre", "fpost"):
            add("%s%d" % (kind, i), 8)
    for j in range(2):
        add("scw%d" % j, 128)
        add("scb%d" % j, 32)
        add("snorm%d" % j, 16)
        add("sdt%d" % j, 1)
        add("lbin%d" % j, 16)
        add("lcw%d" % j, 32)
        add("lcb%d" % j, 8)
        add("lba%d" % j, 8)
        add("lbx%d" % j, 8)
        add("llam%d" % j, 8)
        add("lbout%d" % j, 8)
    return cols, n


VCOLS, NV = build_vec_layout()
CFG = dict(parts=(0, 1, 2, 3), layers=(0, 1, 2, 3), mixer=True, ffn=True, ncore=8, ssd_stop=99, groups=8, nchunks=99)


def build_nc():
    nc = bass.Bass("TRN2", target_bir_lowering=False)

    def din(name, shape):
        return nc.dram_tensor(name, list(shape), F32, kind="ExternalInput").ap()

    def dout(name, shape):
        return nc.dram_tensor(name, list(shape), F32, kind="ExternalOutput").ap()

    xin = din("xin", [128, 8, NPART, TL])
    cst = din("cst", [128, 6, 128])
    vec = din("vec", [128, NV])
    vecb = din("vecb", [1, 2 * 64])
    w_ssd_in = din("w_ssd_in", [2, 8, 2, 128, 8, 384])
    w_ssd_dt = din("w_ssd_dt", [2, 128, 8, 32])
    w_ssd_out = din("w_ssd_out", [2, 4, 128, 16, 256])
    w_lru_in = din("w_lru_in", [2, 8, 128, 8, 256])
    w_lru_ax = din("w_lru_ax", [2, 128, 8, 256])
    w_lru_out = din("w_lru_out", [2, 2, 128, 8, 512])
    w_ffn1 = din("w_ffn1", [4, 8, 128, 8, 512])
    w_ffn2 = din("w_ffn2", [4, 8, 128, 32, 128])
    s_sconv = din("s_sconv", [2, 128, 32, NB, 3])
    s_sh = din("s_sh", [2, NB, 128, 2048])
    s_lconv = din("s_lconv", [2, 128, 8, NB, 3])
    s_lh = din("s_lh", [2, 128, 8, NB])

    yout = dout("yout", [128, 8, NPART, TL])
    o_psconv = dout("o_psconv", [2, 128, 32, 3])
    o_ssconv = dout("o_ssconv", [2, 128, 32, NB, 3])
    o_psh = dout("o_psh", [2, 128, 2048])
    o_ssh = dout("o_ssh", [2, NB, 128, 2048])
    o_plconv = dout("o_plconv", [2, 128, 8, 3])
    o_slconv = dout("o_slconv", [2, 128, 8, NB, 3])
    o_plh = dout("o_plh", [2, 8, 128, 1])
    o_slh = dout("o_slh", [2, 128, 8, NB])

    es = ExitStack()
    with es:
        def sb(name, shape, dt=F32):
            return es.enter_context(nc.sbuf_tensor(name, list(shape), dt))

        S = Sched(nc)
        X = sb("X", [128, 8, TL])
        M = sb("M", [128, 8, TL])
        HB = sb("HB", [128, 8, TL], BF16)
        HT = sb("HT", [128, 2, 2048])
        HTB = sb("HTB", [128, 2048], BF16)
        NRING = 4
        RING = [sb("ring%d" % i, [128, 4096], BF16) for i in range(NRING)]
        CST = sb("CST", [128, 6, 128])
        IDB = sb("IDB", [128, 128], BF16)
        ONEB = sb("ONEB", [128, 128], BF16)
        VEC = sb("VEC", [128, NV])
        ALOGB = sb("ALOGB", [128, 128])
        ANEG = sb("ANEG", [128, 2, 32])
        LSC = sb("LSC", [128, 2, 8])
        SQ = sb("SQ", [128, 8, 264], BF16)
        RS = sb("RS", [128, 2, 264])
        WDT = sb("WDT", [128, 8, 32], BF16)
        WAX = sb("WAX", [128, 8, 256], BF16)
        SCARRY = sb("SCARRY", [128, 2, 32, 3])
        LCARRY = sb("LCARRY", [128, 2, 8, 3])
        HCARRY = sb("HCARRY", [128, 2, 8])
        ARENA_F32 = 23400
        ARENA = sb("ARENA", [128, ARENA_F32])
        PS = [es.enter_context(nc.psum_tensor("ps%d" % i, [128, 512], F32)) for i in range(6)]
        PB = es.enter_context(nc.psum_tensor("psb", [128, 1024], BF16))
        PB2 = es.enter_context(nc.psum_tensor("psb2", [128, 1024], BF16))

        IDF = CST[:, 0, :]
        VM = CST[:, 1, :]
        MLOW = CST[:, 2, :]
        ONEF = CST[:, 3, :]
        I16B = CST[:, 4:6, :]

        st = {"bank": 0, "ring": 0, "uid": 0}

        def pbank():
            b = st["bank"]
            st["bank"] = (b + 1) % CFG.get("nbanks", 6)
            return b

        def vcol(name, off=0, n=1):
            c = VCOLS[name] + off
            return VEC[:, c:c + n]

        def act(out, in_, func, reads, writes, **kw):
            S.op("act", lambda e: e.activation(out=out, in_=in_, func=func, **kw), reads, writes)

        def tt(out, in0, in1, op, reads, writes, eng="dve"):
            S.op(eng, lambda e: e.tensor_tensor(out=out, in0=in0, in1=in1, op=op), reads, writes)

        def ts(out, in0, s1, s2, op0, op1, reads, writes, eng="dve"):
            if op1 is None:
                S.op(eng, lambda e: e.tensor_scalar(out=out, in0=in0, scalar1=s1, scalar2=None, op0=op0), reads, writes)
            else:
                S.op(eng, lambda e: e.tensor_scalar(out=out, in0=in0, scalar1=s1, scalar2=s2, op0=op0, op1=op1), reads, writes)

        def stt(out, in0, scalar, in1, op0, op1, reads, writes):
            S.op("dve", lambda e: e.scalar_tensor_tensor(out=out, in0=in0, scalar=scalar, in1=in1, op0=op0, op1=op1), reads, writes)

        def mm(out, lhsT, rhs, start, stop, reads, writes, inc):
            S.op("pe", lambda e: e.matmul(out, lhsT=lhsT, rhs=rhs, start=start, stop=stop), reads, writes, inc=inc)

        def tr(out, in_, ident, reads, writes, inc=True):
            S.op("pe", lambda e: e.transpose(out, in_, ident), reads, writes, inc=inc)

        def cp(out, in_, reads, writes, eng="dve"):
            if eng == "act":
                act(out, in_, AF.Copy, reads, writes)
            else:
                S.op(eng, lambda e: e.tensor_copy(out=out, in_=in_), reads, writes)

        def wload(src_ap, n1, n2):
            i = st["ring"]
            st["ring"] = (i + 1) % NRING
            view = RING[i][:, 0:n1 * n2].rearrange("p (a b) -> p a b", a=n1)
            key = "ring%d" % i
            S.dma("pool", lambda e: e.dma_start(out=view, in_=src_ap), key, writes=[key])
            return view, key

        S.dma("sp", lambda e: e.dma_start(out=CST[:], in_=cst), "CST", writes=["CST"])
        S.dma("sp", lambda e: e.dma_start(out=VEC[:], in_=vec), "VEC", writes=["VEC"])
        S.dma("sp", lambda e: e.dma_start(out=ALOGB[:], in_=vecb.partition_broadcast(128)), "ALOGB", writes=["ALOGB"])
        cp(IDB[:], IDF, ["CST"], ["IDB"])
        cp(ONEB[:], ONEF, ["CST"], ["ONEB"])
        for j in range(2):
            act(ANEG[:, j, :], ALOGB[:, j * 64:j * 64 + 32], AF.Exp, ["ALOGB"], ["ANEG"])
        ts(ANEG[:], ANEG[:], -1.0, None, ALU.mult, None, ["ANEG"], ["ANEG"])
        for j in range(2):
            act(LSC[:, j, :], vcol("llam%d" % j, 0, 8), AF.Exp, ["VEC"], ["LSC"], scale=-1.0)
        act(LSC[:], LSC[:], AF.Ln, ["LSC"], ["LSC"], bias=1.0)
        ts(LSC[:], LSC[:], -8.0, None, ALU.mult, None, ["LSC"], ["LSC"])
        VECH = sb("VECH", [128, NV])
        ts(VECH[:], VEC[:], 0.5, None, ALU.mult, None, ["VEC"], ["VECH"])
        NHALF = sb("NHALF", [128, 2])
        S.op("dve", lambda e: e.memset(NHALF[:], -0.5), [], ["NHALF"])
        S.op("dve", lambda e: e.memset(SCARRY[:], 0.0), [], ["SCARRY%d_%d" % (a_, b_) for a_ in range(2) for b_ in range(32)])
        S.op("dve", lambda e: e.memset(LCARRY[:], 0.0), [], ["LCARRY%d_%d" % (a_, b_) for a_ in range(2) for b_ in range(8)])
        S.op("dve", lambda e: e.memset(HCARRY[:], 0.0), [], ["HCARRY%d_%d" % (a_, b_) for a_ in range(2) for b_ in range(8)])
        S.op("dve", lambda e: e.memset(HT[:], 0.0), [], ["HT%d_%d" % (jj, gg) for jj in range(2) for gg in range(8)])

        def dskb(j, g):
            return ALOGB[:, j * 64 + 32 + 4 * g: j * 64 + 32 + 4 * g + 4]

        def rstd_for(src, srckeys, ti):
            off, n = TTS[ti]
            b = pbank()
            pk = "P%d" % b
            for kc in range(8):
                sq = SQ[:, kc, 0:n]
                sk = "SQ%d" % kc
                act(sq, src[:, kc, off:off + n], AF.Square, srckeys, [sk])
                mm(PS[b][:, 0:n], ONEB[:], sq, kc == 0, kc == 7, ["ONEB", sk], [pk], inc=(kc == 7))
            rs = RS[:, ti % 2, 0:n]
            rk = "RS%d" % (ti % 2)
            act(rs, PS[b][:, 0:n], AF.Sqrt, [pk, "EPS"], [rk], scale=1.0 / D, bias=EPS_AP[:, 0:1])
            S.op("dve", lambda e: e.reciprocal(out=rs, in_=rs), [rk], [rk])
            return rs, rk

        EPS_T = sb("EPS_T", [128, 2])
        EPS_AP = EPS_T
        S.op("dve", lambda e: e.memset(EPS_T[:], EPS), [], ["EPS"])

        def prenorm(gname, srcdep):
            for ti, (off, n) in enumerate(TTS):
                rs, rk = rstd_for(X, ["X%d" % ti] , ti)
                for kc in range(8):
                    stt(HB[:, kc, off:off + n], X[:, kc, off:off + n], vcol(gname, kc), rs, ALU.mult, ALU.mult,
                        ["X%d" % ti, rk, "VEC", "EPS"], ["HB%d" % ti])

        def postnorm(gname):
            for ti, (off, n) in enumerate(TTS):
                rs, rk = rstd_for(M, ["M0", "M1"], ti)
                for kc in range(8):
                    stt(M[:, kc, off:off + n], M[:, kc, off:off + n], vcol(gname, kc), rs, ALU.mult, ALU.mult,
                        ["M%d" % ti, rk, "VEC", "EPS"], ["M%d" % ti])
                    tt(X[:, kc, off:off + n], X[:, kc, off:off + n], M[:, kc, off:off + n], ALU.add,
                       ["M%d" % ti, "X%d" % ti], ["X%d" % ti])

        HBK = ["HB0", "HB1"]

        def ffn(layer):
            HID = ARENA[:, 0:8448].bitcast(BF16).rearrange("p (a b) -> p a b", a=32)
            R = ARENA[:, 8448:8448 + 528].rearrange("p (a b) -> p a b", a=2)
            for q in range(8):
                W, wk = wload(w_ffn1[layer, q], 8, 512)
                for jj in range(4):
                    j = 4 * q + jj
                    for ti, (off, n) in enumerate(TTS):
                        b = pbank()
                        pk = "P%d" % b
                        for kc in range(8):
                            mm(PS[b][:, 0:n], W[:, kc, jj * 128:(jj + 1) * 128], HB[:, kc, off:off + n], kc == 0, kc == 7,
                               [wk, "HB%d" % ti], [pk], inc=(kc == 7))
                        r = R[:, ti, 0:n]
                        act(r, PS[b][:, 0:n], AF.Relu, [pk], ["R%d" % ti])
                        tt(HID[:, j, off:off + n], r, r, ALU.mult, ["R%d" % ti], ["HID%d_%d" % (j, ti)])
            for c in range(8):
                W, wk = wload(w_ffn2[layer, c], 32, 128)
                for ti, (off, n) in enumerate(TTS):
                    b = pbank()
                    pk = "P%d" % b
                    for j in range(32):
                        mm(PS[b][:, 0:n], W[:, j, :], HID[:, j, off:off + n], j == 0, j == 31,
                           [wk, "HID%d_%d" % (j, ti)], [pk], inc=(j == 31))
                    cp(M[:, c, off:off + n], PS[b][:, 0:n], [pk], ["M%d" % ti], eng="act")

        def run_streams(factories, nslots, extra=None, stag=0):
            pending = list(factories)
            free = list(range(nslots))
            active = []
            extra = list(extra or [])
            rounds = 0
            last_start = -10 ** 9
            while pending or active or extra:
                rounds += 1
                while pending and free and (not active or rounds - last_start >= stag):
                    s_ = free.pop(0)
                    active.append((pending.pop(0)(s_), s_))
                    last_start = rounds
                for item in list(active):
                    gen, s_ = item
                    try:
                        next(gen)
                    except StopIteration:
                        active.remove(item)
                        free.append(s_)
                for gen in list(extra):
                    try:
                        next(gen)
                    except StopIteration:
                        extra.remove(gen)

        def run_all(gen):
            for _ in gen:
                pass

        def conv_block(pc, pck, outf, outk, wname, woff, bname, boff, W, sample_prev, prevk, tab=None):
            tabv = VEC if tab is None else tab
            tabk = "VEC" if tab is None else "VECH"
            w = [tabv[:, VCOLS[wname] + woff + k:VCOLS[wname] + woff + k + 1] for k in range(4)]
            bb = tabv[:, VCOLS[bname] + boff:VCOLS[bname] + boff + 1]
            act(outf[:, 0:W], pc[:, 0:W], AF.Identity, [pck, tabk], [outk], scale=w[0], bias=bb)
            yield
            for k in (1, 2, 3):
                stt(outf[:, 0:W], pc[:, k:k + W], w[k], outf[:, 0:W], ALU.mult, ALU.add, [pck, tabk, outk], [outk])
                yield
            if sample_prev is not None:
                o = outf[:, SEQ:TL]
                act(o, pc[:, 3 + SEQ:3 + TL], AF.Identity, [pck, tabk], [outk], scale=w[3], bias=bb)
                for k in range(3):
                    stt(o, sample_prev[:, :, k], w[k], o, ALU.mult, ALU.add, [prevk, tabk, outk], [outk])
                yield

        def lru(j, part):
            kind = ["seq", "none", "none", "sample"][part]
            W_ = TL if kind == "seq" else SEQ
            Y = ARENA[:, 0:2112].bitcast(BF16).rearrange("p (a b) -> p a b", a=8)
            o = 2112

            def carve(nf32):
                nonlocal o
                v = ARENA[:, o:o + nf32]
                o += nf32
                return v
            NSL = CFG.get("lru_slots", 3)
            T = []
            for s_ in range(NSL):
                d = {}
                d["PC"] = carve(532)[:, 0:531]
                for nm in ("G", "XR", "Rg", "Ig", "Aa", "Tm", "H"):
                    d[nm] = carve(528)
                d["XRB"] = carve(264).bitcast(BF16)
                d["PREV"] = carve(48).rearrange("p (a b) -> p a b", a=NB)
                d["H0S"] = carve(16)
                d["OLC"] = carve(48).rearrange("p (a b) -> p a b", a=NB)
                T.append(d)
            assert o <= ARENA_F32, o
            S.dma("pool", lambda e: e.dma_start(out=WAX[:], in_=w_lru_ax[j]), "WAX", reads=HBK, writes=["WAX"])

            def block(k, s_):
                d = T[s_]
                PC, G, XR, XRB, Rg, Ig, Aa, Tm, H = d["PC"], d["G"], d["XR"], d["XRB"], d["Rg"], d["Ig"], d["Aa"], d["Tm"], d["H"]
                PREV, H0S, OLC = d["PREV"], d["H0S"], d["OLC"]
                K = lambda n: "%s_%d" % (n, s_)
                Wt, wk = wload(w_lru_in[j, k], 8, 256)
                cp(PC[:, 0:3], LCARRY[:, j, k, :], ["LCARRY%d_%d" % (j, k)] + HBK, [K("PC")])
                if kind == "sample":
                    S.dma("sp", lambda e: e.dma_start(out=PREV, in_=s_lconv[j, :, k]), K("PREVL"), reads=HBK, writes=[K("PREVL")])
                    S.dma("sp", lambda e: e.dma_start(out=H0S, in_=s_lh[j, :, k]), K("H0S"), reads=HBK, writes=[K("H0S")])
                yield
                for ti, (off, n) in enumerate(TTS):
                    b = pbank()
                    pk = "P%d" % b
                    for kc in range(8):
                        mm(PS[b][:, 0:n], Wt[:, kc, 0:128], HB[:, kc, off:off + n], kc == 0, kc == 7, [wk, "HB%d" % ti], [pk], inc=(kc == 7))
                    act(G[:, off:off + n], PS[b][:, 0:n], AF.Identity, [pk, "VEC"], [K("G")], bias=vcol("lbin%d" % j, k))
                    yield
                    b = pbank()
                    pk = "P%d" % b
                    for kc in range(8):
                        mm(PS[b][:, 0:n], Wt[:, kc, 128:256], HB[:, kc, off:off + n], kc == 0, kc == 7, [wk, "HB%d" % ti], [pk], inc=(kc == 7))
                    act(PC[:, 3 + off:3 + off + n], PS[b][:, 0:n], AF.Identity, [pk, "VEC"], [K("PC")], bias=vcol("lbin%d" % j, 8 + k))
                    yield
                tt(Tm, G, G, ALU.mult, [K("G")], [K("Tm")])
                yield
                ts(Tm, Tm, 0.044715, 1.0, ALU.mult, ALU.add, [K("Tm")], [K("Tm")])
                yield
                tt(Tm, Tm, G, ALU.mult, [K("Tm"), K("G")], [K("Tm")])
                yield
                act(Tm, Tm, AF.Sigmoid, [K("Tm")], [K("Tm")], scale=1.5957691216)
                yield
                tt(G, G, Tm, ALU.mult, [K("G"), K("Tm")], [K("G")])
                yield
                yield from conv_block(PC, K("PC"), XR, K("XR"), "lcw%d" % j, 4 * k, "lcb%d" % j, k, W_,
                                      PREV if kind == "sample" else None, K("PREVL"))
                cp(LCARRY[:, j, k, :], PC[:, W_:W_ + 3], [K("PC")], ["LCARRY%d_%d" % (j, k)])
                if kind == "sample":
                    cp(OLC[:, :, 0:2], PREV[:, :, 1:3], [K("PREVL")], [K("OLC")])
                    cp(OLC[:, :, 2], PC[:, 3 + SEQ:3 + TL], [K("PC")], [K("OLC")])
                    S.dma("sp", lambda e: e.dma_start(out=o_slconv[j, :, k], in_=OLC), K("OLCD"), reads=[K("OLC")])
                    S.dma("sp", lambda e: e.dma_start(out=o_plconv[j, :, k], in_=PC[:, SEQ:SEQ + 3]), K("OPLC"), reads=[K("PC")])
                cp(XRB, XR, [K("XR")], [K("XRB")], eng="act")
                yield
                for ti, (off, n) in enumerate(TTS):
                    b = pbank()
                    pk = "P%d" % b
                    mm(PS[b][:, 0:n], WAX[:, k, 0:128], XRB[:, off:off + n], True, True, ["WAX", K("XRB")], [pk], inc=True)
                    act(Rg[:, off:off + n], PS[b][:, 0:n], AF.Sigmoid, [pk, "VEC"], [K("Rg")], bias=vcol("lba%d" % j, k))
                    yield
                    b = pbank()
                    pk = "P%d" % b
                    mm(PS[b][:, 0:n], WAX[:, k, 128:256], XRB[:, off:off + n], True, True, ["WAX", K("XRB")], [pk], inc=True)
                    act(Ig[:, off:off + n], PS[b][:, 0:n], AF.Sigmoid, [pk, "VEC"], [K("Ig")], bias=vcol("lbx%d" % j, k))
                    yield
                act(Aa, Rg, AF.Exp, [K("Rg"), "LSC"], [K("Aa")], scale=LSC[:, j, k:k + 1])
                yield
                tt(Tm, Aa, Aa, ALU.mult, [K("Aa")], [K("Tm")])
                yield
                act(Tm, Tm, AF.Relu, [K("Tm")], [K("Tm")], scale=-1.0, bias=1.0)
                yield
                act(Tm, Tm, AF.Sqrt, [K("Tm")], [K("Tm")])
                yield
                tt(Tm, Tm, Ig, ALU.mult, [K("Tm"), K("Ig")], [K("Tm")])
                yield
                tt(Tm, Tm, XR, ALU.mult, [K("Tm"), K("XR")], [K("Tm")])
                yield
                S.op("dve", lambda e: e.tensor_tensor_scan(out=H[:, 0:W_], data0=Aa[:, 0:W_], data1=Tm[:, 0:W_],
                                                           initial=HCARRY[:, j, k:k + 1], op0=ALU.mult, op1=ALU.add),
                     [K("Aa"), K("Tm"), "HCARRY%d_%d" % (j, k)], [K("H")])
                yield
                cp(HCARRY[:, j, k:k + 1], H[:, W_ - 1:W_], [K("H")], ["HCARRY%d_%d" % (j, k)])
                if kind == "sample":
                    tt(H[:, SEQ:TL], Aa[:, SEQ:TL], H0S, ALU.mult, [K("Aa"), K("H0S")], [K("H")])
                    tt(H[:, SEQ:TL], H[:, SEQ:TL], Tm[:, SEQ:TL], ALU.add, [K("H"), K("Tm")], [K("H")])
                    S.dma("sp", lambda e: e.dma_start(out=o_slh[j, :, k], in_=H[:, SEQ:TL]), K("OSLH"), reads=[K("H")])
                    S.dma("sp", lambda e: e.dma_start(out=o_plh[j, k], in_=H[:, SEQ - 1:SEQ]), K("OPLH"), reads=[K("H")])
                elif kind == "none":
                    S.op("dve", lambda e: e.memset(H[:, SEQ:TL], 0.0), [K("H")], [K("H")])
                yield
                tt(Y[:, k, :], H, G, ALU.mult, [K("H"), K("G")], ["Y%d" % k])
                yield

            run_streams([(lambda s_, k=k: block(k, s_)) for k in range(8)], NSL, stag=CFG.get("lru_stag", 0))
            for q in range(2):
                Wt, wk = wload(w_lru_out[j, q], 8, 512)
                for cc in range(4):
                    c = 4 * q + cc
                    for ti, (off, n) in enumerate(TTS):
                        b = pbank()
                        pk = "P%d" % b
                        for kc in range(8):
                            mm(PS[b][:, 0:n], Wt[:, kc, cc * 128:(cc + 1) * 128], Y[:, kc, off:off + n], kc == 0, kc == 7,
                               [wk, "Y%d" % kc], [pk], inc=(kc == 7))
                        act(M[:, c, off:off + n], PS[b][:, 0:n], AF.Identity, [pk, "VEC"], ["M%d" % ti], bias=vcol("lbout%d" % j, c))

        def ssd(j, part):
            kind = ["seq", "none", "none", "sample"][part]
            W_ = TL if kind == "seq" else SEQ
            nch = 5 if kind == "seq" else 4
            o = 0

            def carve(nf32):
                nonlocal o
                v = ARENA[:, o:o + nf32]
                o += nf32
                return v
            YF = carve(4224).bitcast(BF16).rearrange("p (a b) -> p a b", a=16)
            XBC2 = [carve(1056).bitcast(BF16).rearrange("p (a b) -> p a b", a=4) for _ in range(3)]
            ZS2 = [carve(528).bitcast(BF16).rearrange("p (a b) -> p a b", a=2) for _ in range(3)]
            PC = carve(532)[:, 0:531]
            CV = carve(528)
            TZ = carve(528)
            DT = carve(528)
            DTM = carve(160).rearrange("p (a b) -> p a b", a=5)
            ATM = carve(160).rearrange("p (a b) -> p a b", a=5)
            ECS = carve(160).rearrange("p (a b) -> p a b", a=5)
            DTE = carve(160).rearrange("p (a b) -> p a b", a=5)
            CD = carve(160).rearrange("p (a b) -> p a b", a=5)
            AMASK = carve(512).rearrange("p (a b) -> p a b", a=NB)
            DECB = carve(512).rearrange("p (a b) -> p a b", a=NB)
            NSL = 2
            TS_ = []
            for s_ in range(NSL):
                d = {}
                for nm in ("XTM", "XDT", "XDTE", "ZTM", "Y3"):
                    d[nm] = carve(128).bitcast(BF16)
                d["BTM"] = carve(64).bitcast(BF16)
                d["CBM"] = carve(64).bitcast(BF16)
                d["AL"] = carve(512).rearrange("p (a b) -> p a b", a=4)
                d["ED"] = carve(256).bitcast(BF16).rearrange("p (a b) -> p a b", a=4)
                d["SC"] = carve(256).bitcast(BF16).rearrange("p (a b) -> p a b", a=4)
                for nm in ("Y1", "Y2", "JK"):
                    d[nm] = carve(256)
                d["SS"] = carve(2)
                TS_.append(d)
            CMASK = carve(256).rearrange("p (a b) -> p a b", a=NB)
            BMASK = carve(1024).bitcast(BF16).rearrange("p (a b) -> p a b", a=NB)
            SIN = carve(1024).rearrange("p (a b) -> p a b", a=4)
            SNEW = carve(1024).rearrange("p (a b) -> p a b", a=4)
            PREV = carve(48).rearrange("p (a b) -> p a b", a=NB)
            OSC = carve(48).rearrange("p (a b) -> p a b", a=NB)
            assert o <= ARENA_F32, o
            hk = "HT%d" % j
            hkg = lambda g: "HT%d_%d" % (j, g)
            hbg = lambda g: "HTB_%d" % g
            allhk = [hkg(g) for g in range(8)]
            aneg = ANEG[:, j, :]

            S.dma("pool", lambda e: e.dma_start(out=WDT[:], in_=w_ssd_dt[j]), "WDT", reads=HBK, writes=["WDT"])
            cp(HTB[:], HT[:, j, :], allhk + HBK, [hbg(g) for g in range(8)], eng="act")
            for ti, (off, n) in enumerate(TTS):
                b = pbank()
                pk = "P%d" % b
                for kc in range(8):
                    mm(PS[b][0:32, 0:n], WDT[:, kc, :], HB[:, kc, off:off + n], kc == 0, kc == 7, ["WDT", "HB%d" % ti], [pk], inc=(kc == 7))
                act(DT[0:32, off:off + n], PS[b][0:32, 0:n], AF.Exp, [pk, "VEC"], ["DT"], bias=VEC[0:32, VCOLS["sdt%d" % j]:VCOLS["sdt%d" % j] + 1])
            act(DT[0:32, :], DT[0:32, :], AF.Ln, ["DT"], ["DT"], bias=1.0)
            chunks = [(c * 128, 128) for c in range(4)] + [(SEQ, NTAIL)]
            ncht = 5 if kind != "none" else 4
            for ch in range(ncht):
                c0, Q = chunks[ch]
                b = pbank()
                pk = "P%d" % b
                tr(PS[b][0:Q, 0:32], DT[0:32, c0:c0 + Q], IDF[0:32, 0:32], ["DT", "CST"], [pk])
                cp(DTM[0:Q, ch, :], PS[b][0:Q, 0:32], [pk], ["DTM"])
                tt(ATM[0:Q, ch, :], DTM[0:Q, ch, :], aneg[0:Q, :], ALU.mult, ["DTM", "ANEG"], ["ATM"])
            for ch in range(nch):
                c0, Q = chunks[ch]
                for (dst, lhs, dk) in ((ECS, VM, "ECS"), (DTE, MLOW, "DTE")):
                    b = pbank()
                    pk = "P%d" % b
                    mm(PS[b][0:Q, 0:32], lhs[0:Q, 0:Q], ATM[0:Q, ch, :], True, True, ["CST", "ATM"], [pk], inc=True)
                    act(dst[0:Q, ch, :], PS[b][0:Q, 0:32], AF.Exp, [pk], [dk])
                b = pbank()
                pk = "P%d" % b
                mm(PS[b][:, 0:32], ONEF[0:Q, :], ATM[0:Q, ch, :], True, True, ["CST", "ATM"], [pk], inc=True)
                act(CD[:, ch, :], PS[b][:, 0:32], AF.Exp, [pk], ["CD"])
            if kind == "sample":
                tt(AMASK[0:NB], ATM[0:NB, 4, :].unsqueeze(1).to_broadcast([NB, NB, 32]),
                   IDF[0:NB, 0:NB].unsqueeze(2).to_broadcast([NB, NB, 32]), ALU.mult, ["ATM", "CST"], ["AMASK"])
                b = pbank()
                pk = "P%d" % b
                mm(PS[b][:, 0:512], ONEF[0:NB, :], AMASK[0:NB].rearrange("p a b -> p (a b)"), True, True, ["CST", "AMASK"], [pk], inc=True)
                act(DECB.rearrange("p a b -> p (a b)"), PS[b][:, 0:512], AF.Exp, [pk], ["DECB"])

            def gen_inproj(g):
                XBC, ZS = XBC2[g % 3], ZS2[g % 3]
                xk, zk = "XBC%d" % (g % 3), "ZS%d" % (g % 3)
                Wa, wka = wload(w_ssd_in[j, g, 0], 8, 384)
                Wb, wkb = wload(w_ssd_in[j, g, 1], 8, 384)
                blocks = [(Wa, wka, 0), (Wa, wka, 128), (Wa, wka, 256), (Wb, wkb, 0), (Wb, wkb, 128), (Wb, wkb, 256)]
                ccs = [2 * g, 2 * g + 1, 16 + g, 24 + g]
                for fb in range(6):
                    Wt, wk, wo = blocks[fb]
                    if fb >= 2:
                        ci = fb - 2
                        cc = ccs[ci]
                        cp(PC[:, 0:3], SCARRY[:, j, cc, :], ["SCARRY%d_%d" % (j, cc)] + HBK, ["PC"])
                        if kind == "sample":
                            S.dma("sp", lambda e, cc=cc: e.dma_start(out=PREV, in_=s_sconv[j, :, cc]), "PREVS", reads=HBK, writes=["PREVS"])
                    for ti, (off, n) in enumerate(TTS):
                        b = pbank()
                        pk = "P%d" % b
                        for kc in range(8):
                            mm(PS[b][:, 0:n], Wt[:, kc, wo:wo + 128], HB[:, kc, off:off + n], kc == 0, kc == 7, [wk, "HB%d" % ti], [pk], inc=(kc == 7))
                        if fb < 2:
                            act(TZ[:, off:off + n], PS[b][:, 0:n], AF.Tanh, [pk], ["TZ"], scale=0.5)
                            stt(ZS[:, fb, off:off + n], TZ[:, off:off + n], 1.0, PS[b][:, 0:n], ALU.add, ALU.mult, ["TZ", pk], [zk])
                        else:
                            cp(PC[:, 3 + off:3 + off + n], PS[b][:, 0:n], [pk], ["PC"], eng="act")
                        yield
                    if fb >= 2:
                        yield from conv_block(PC, "PC", CV, "CV", "scw%d" % j, 4 * cc, "scb%d" % j, cc, W_,
                                              PREV if kind == "sample" else None, "PREVS", tab=VECH)
                        cp(SCARRY[:, j, cc, :], PC[:, W_:W_ + 3], ["PC"], ["SCARRY%d_%d" % (j, cc)])
                        if kind == "sample":
                            cp(OSC[:, :, 0:2], PREV[:, :, 1:3], ["PREVS"], ["OSC"])
                            cp(OSC[:, :, 2], PC[:, 3 + SEQ:3 + TL], ["PC"], ["OSC"])
                            S.dma("sp", lambda e, cc=cc: e.dma_start(out=o_ssconv[j, :, cc], in_=OSC), "OSCD", reads=["OSC"])
                            S.dma("sp", lambda e, cc=cc: e.dma_start(out=o_psconv[j, :, cc], in_=PC[:, SEQ:SEQ + 3]), "OPSC", reads=["PC"])
                        if kind == "none":
                            S.op("dve", lambda e: e.memset(CV[:, SEQ:TL], 0.0), ["CV"], ["CV"])
                        act(TZ[:, :], CV, AF.Tanh, ["CV"], ["TZ"])
                        yield
                        stt(XBC[:, ci, :], TZ[:, :], 1.0, CV, ALU.add, ALU.mult, ["TZ", "CV"], [xk])
                        yield

            state_ready = {}

            def gen_chunk(g, ch, s_):
                d = TS_[s_]
                XTM, XDT, XDTE, ZTM, BTM, CBM, AL, ED, SC, Y1, Y2, JK, Y3, SS = (d[n] for n in
                    ("XTM", "XDT", "XDTE", "ZTM", "BTM", "CBM", "AL", "ED", "SC", "Y1", "Y2", "JK", "Y3", "SS"))
                K = lambda n: "%s_%d" % (n, s_)
                XBC, ZS = XBC2[g % 3], ZS2[g % 3]
                xk, zk = "XBC%d" % (g % 3), "ZS%d" % (g % 3)
                gs = slice(4 * g, 4 * g + 4)
                c0, Q = chunks[ch]
                decode = (ch == 4 and kind == "sample")
                cols = slice(c0, c0 + Q)
                r4 = lambda ap: ap.rearrange("p (a b) -> p a b", a=4)
                pk1 = "PBf"
                pv = PB[:, 0:640]
                for i in range(2):
                    tr(pv[0:Q, i * 128:(i + 1) * 128], XBC[:, i, cols], IDB[:], [xk, "IDB"], [pk1], inc=False)
                for i in range(2):
                    tr(pv[0:Q, 256 + i * 128:256 + (i + 1) * 128], ZS[:, i, cols], IDB[:], [zk, "IDB"], [pk1], inc=False)
                tr(pv[0:Q, 512:640], XBC[:, 2, cols], IDB[:], [xk, "IDB"], [pk1], inc=True)
                cp(XTM[0:Q, :], pv[0:Q, 0:256], [pk1], [K("XTM")], eng="act")
                act(ZTM[0:Q, :], pv[0:Q, 256:512], AF.Copy, [pk1], [K("ZTM")], scale=0.5)
                cp(BTM[0:Q, :], pv[0:Q, 512:640], [pk1], [K("BTM")], eng="act")
                yield
                tt(r4(XDT[0:Q, :]), r4(XTM[0:Q, :]), DTM[0:Q, ch, gs].unsqueeze(2).to_broadcast([Q, 4, 64]), ALU.mult, [K("XTM"), "DTM"], [K("XDT")])
                yield
                if not decode:
                    tt(r4(XDTE[0:Q, :]), r4(XDT[0:Q, :]), DTE[0:Q, ch, gs].unsqueeze(2).to_broadcast([Q, 4, 64]), ALU.mult, [K("XDT"), "DTE"], [K("XDTE")])
                    yield
                    while state_ready.get(g, 0) != ch:
                        yield
                    bO = pbank()
                    pkO = "P%d" % bO
                    mm(PS[bO][0:Q, 0:256], XBC[:, 3, cols], HTB[:, g * 256:(g + 1) * 256], True, True, [xk, hbg(g)], [pkO], inc=True)
                    tt(r4(Y1[0:Q, :]), r4(PS[bO][0:Q, 0:256]), ECS[0:Q, ch, gs].unsqueeze(2).to_broadcast([Q, 4, 64]), ALU.mult, [pkO, "ECS"], [K("Y1")])
                    yield
                    htg = HT[:, j, g * 256:(g + 1) * 256]
                    tt(r4(htg), r4(htg), CD[:, ch, gs].unsqueeze(2).to_broadcast([128, 4, 64]), ALU.mult, [hkg(g), "CD"], [hkg(g)])
                    b4 = pbank()
                    pk4 = "P%d" % b4
                    mm(PS[b4][:, 0:256], BTM[0:Q, :], XDTE[0:Q, :], True, True, [K("BTM"), K("XDTE")], [pk4], inc=True)
                    tt(htg, htg, PS[b4][:, 0:256], ALU.add, [hkg(g), pk4], [hkg(g)])
                    cp(HTB[:, g * 256:(g + 1) * 256], htg, [hkg(g)], [hbg(g)], eng="act")
                    state_ready[g] = ch + 1
                    yield
                    b2 = pbank()
                    pk2 = "P%d" % b2
                    mm(PS[b2][0:Q, 0:Q], XBC[:, 2, cols], XBC[:, 3, cols], True, True, [xk], [pk2], inc=True)
                    tt(CBM[0:Q, 0:Q], PS[b2][0:Q, 0:Q], VM[0:Q, 0:Q], ALU.mult, [pk2, "CST"], [K("CBM")])
                    yield
                    tt(AL[0:Q, :, 0:Q], MLOW[0:Q, 0:Q].unsqueeze(1).to_broadcast([Q, 4, Q]),
                       ATM[0:Q, ch, gs].unsqueeze(2).to_broadcast([Q, 4, Q]), ALU.mult, ["CST", "ATM"], [K("AL")])
                    yield
                    b3 = pbank()
                    pk3 = "P%d" % b3
                    for h4 in range(4):
                        mm(PS[b3][0:Q, h4 * 128:h4 * 128 + Q], AL[0:Q, h4, 0:Q], VM[0:Q, 0:Q], True, True, [K("AL"), "CST"], [pk3], inc=(h4 == 3))
                    act(ED[0:Q, :, 0:Q], PS[b3][0:Q, :].rearrange("p (a b) -> p a b", a=4)[:, :, 0:Q], AF.Exp, [pk3], [K("ED")])
                    yield
                    tt(SC[0:Q, :, 0:Q], ED[0:Q, :, 0:Q], CBM[0:Q, 0:Q].unsqueeze(1).to_broadcast([Q, 4, Q]), ALU.mult, [K("ED"), K("CBM")], [K("SC")])
                    yield
                    XD = JK.bitcast(BF16)[:, 0:256]
                    tt(r4(XD[0:Q, :]), r4(XTM[0:Q, :]), dskb(j, g)[0:Q, :].unsqueeze(2).to_broadcast([Q, 4, 64]), ALU.mult, [K("XTM"), "ALOGB"], [K("JK")])
                    yield
                    bY = pbank()
                    pkY = "P%d" % bY
                    mm(PS[bY][0:Q, 0:256], IDB[0:Q, 0:Q], XD[0:Q, :], True, False, ["IDB", K("JK")], [pkY], inc=False)
                    for h4 in range(4):
                        mm(PS[bY][0:Q, h4 * 64:(h4 + 1) * 64], SC[0:Q, h4, 0:Q], XDT[0:Q, h4 * 64:(h4 + 1) * 64], False, h4 == 3,
                           [K("SC"), K("XDT")], [pkY], inc=(h4 == 3))
                    tt(Y1[0:Q, :], Y1[0:Q, :], PS[bY][0:Q, 0:256], ALU.add, [K("Y1"), pkY], [K("Y1")])
                    yield
                else:
                    bY = pbank()
                    pkY = "P%d" % bY
                    tt(CMASK[:, :, :], XBC[:, 3, cols].unsqueeze(2).to_broadcast([128, NB, NB]),
                       I16B.rearrange("p a b -> p (a b)").rearrange("p (a b) -> p a b", a=NB), ALU.mult, [xk, "CST"], ["CMASK"])
                    tt(BMASK[0:NB], BTM[0:NB, :].unsqueeze(1).to_broadcast([NB, NB, 128]),
                       IDF[0:NB, 0:NB].unsqueeze(2).to_broadcast([NB, NB, 128]), ALU.mult, [K("BTM"), "CST"], ["BMASK"])
                    for bi in range(NB):
                        sl = bi % 4
                        sk, nk = "SIN%d" % sl, "SNEW%d" % sl
                        S.dma("sp", lambda e, bi=bi, sl=sl: e.dma_start(out=SIN[:, sl, :], in_=s_sh[j, bi, :, g * 256:(g + 1) * 256]),
                              sk, reads=HBK, writes=[sk])
                        tt(r4(SNEW[:, sl, :]), r4(SIN[:, sl, :]), DECB[:, bi, gs].unsqueeze(2).to_broadcast([128, 4, 64]), ALU.mult, [sk, "DECB"], [nk])
                        b5 = pbank()
                        if b5 == bY:
                            b5 = pbank()
                        pk5 = "P%d" % b5
                        mm(PS[b5][:, 0:256], BMASK[0:NB, bi, :], XDT[0:NB, :], True, True, ["BMASK", K("XDT")], [pk5], inc=True)
                        tt(SNEW[:, sl, :], SNEW[:, sl, :], PS[b5][:, 0:256], ALU.add, [nk, pk5], [nk])
                        S.dma("pool", lambda e, bi=bi, sl=sl: e.dma_start(out=o_ssh[j, bi, :, g * 256:(g + 1) * 256], in_=SNEW[:, sl, :]),
                              "O" + nk, reads=[nk])
                        mm(PS[bY][0:NB, 0:256], CMASK[:, bi, :], SNEW[:, sl, :], bi == 0, bi == NB - 1, ["CMASK", nk], [pkY], inc=True)
                    cp(Y1[0:Q, :], PS[bY][0:Q, 0:256], [pkY], [K("Y1")])
                    yield
                    tt(r4(JK[0:Q, :]), r4(XTM[0:Q, :]), dskb(j, g)[0:Q, :].unsqueeze(2).to_broadcast([Q, 4, 64]), ALU.mult, [K("XTM"), "ALOGB"], [K("JK")])
                    yield
                    tt(Y1[0:Q, :], Y1[0:Q, :], JK[0:Q, :], ALU.add, [K("Y1"), K("JK")], [K("Y1")])
                    yield
                tt(Y2[0:Q, :], Y1[0:Q, :], ZTM[0:Q, :], ALU.mult, [K("Y1"), K("ZTM")], [K("Y2")])
                yield
                tt(JK[0:Q, :], Y2[0:Q, :], Y2[0:Q, :], ALU.mult, [K("Y2"), K("JK")], [K("JK")])
                yield
                S.op("dve", lambda e: e.tensor_reduce(out=SS[0:Q, 0:1], in_=JK[0:Q, :], axis=AX.X, op=ALU.add), [K("JK")], [K("SS")])
                yield
                ts(SS[0:Q, 0:1], SS[0:Q, 0:1], 1.0 / 256, EPS, ALU.mult, ALU.add, [K("SS")], [K("SS")], eng="pool")
                tt(SS[0:Q, 0:1], SS[0:Q, 0:1], NHALF[0:Q, 0:1], ALU.pow, [K("SS"), "NHALF"], [K("SS")], eng="pool")
                yield
                act(Y3[0:Q, :], Y2[0:Q, :], AF.Identity, [K("Y2"), K("SS")], [K("Y3")], scale=SS[0:Q, 0:1])
                yield
                pk6 = "PBb"
                pv6 = PB2[:, 0:256]
                for i in range(2):
                    tr(pv6[:, i * 128:i * 128 + Q], Y3[0:Q, i * 128:(i + 1) * 128], IDB[0:Q, 0:Q], [K("Y3"), "IDB"], [pk6], inc=(i == 1))
                for i in range(2):
                    act(YF[:, 2 * g + i, cols], pv6[:, i * 128:i * 128 + Q], AF.Identity, [pk6, "VEC"], ["YF%d" % (2 * g + i)],
                        scale=vcol("snorm%d" % j, 2 * g + i))
                yield

            run_all(gen_inproj(0))
            inproj_done = {0: True}
            spawned = {0}

            def wrap_inproj(g):
                yield from gen_inproj(g)
                inproj_done[g] = True

            pending = [(g, ch) for g in range(8) for ch in range(ncht)]
            active = []
            free = list(range(NSL))
            extra = []
            xsteps = CFG.get("extra_steps", 1)
            while pending or active or extra:
                while pending and free and inproj_done.get(pending[0][0]):
                    g, ch = pending.pop(0)
                    s_ = free.pop(0)
                    active.append((gen_chunk(g, ch, s_), s_))
                    if g + 1 < 8 and (g + 1) not in spawned:
                        spawned.add(g + 1)
                        extra.append(wrap_inproj(g + 1))
                for item in list(active):
                    try:
                        next(item[0])
                    except StopIteration:
                        active.remove(item)
                        free.append(item[1])
                for gen in list(extra):
                    try:
                        for _ in range(xsteps):
                            next(gen)
                    except StopIteration:
                        extra.remove(gen)
            if kind == "none":
                for c16 in range(16):
                    S.op("dve", lambda e, c16=c16: e.memset(YF[:, c16, SEQ:TL], 0.0), ["YF%d" % c16], ["YF%d" % c16])
            if part == NPART - 1:
                S.dma("sp", lambda e: e.dma_start(out=o_psh[j], in_=HT[:, j, :]), "OPSH%d" % j, reads=allhk)
            for q in range(4):
                Wt, wk = wload(w_ssd_out[j, q], 16, 256)
                for cc2 in range(2):
                    c = 2 * q + cc2
                    for ti, (off, n) in enumerate(TTS):
                        b = pbank()
                        pk = "P%d" % b
                        for kc in range(16):
                            mm(PS[b][:, 0:n], Wt[:, kc, cc2 * 128:(cc2 + 1) * 128], YF[:, kc, off:off + n], kc == 0, kc == 15,
                               [wk, "YF%d" % kc], [pk], inc=(kc == 15))
                        cp(M[:, c, off:off + n], PS[b][:, 0:n], [pk], ["M%d" % ti], eng="act")

        XK = ["X0", "X1"]
        for part in CFG["parts"]:
            S.dma("sp", lambda e, part=part: e.dma_start(out=X[:], in_=xin[:, :, part, :]), "XLD", writes=XK)
            for layer in CFG["layers"]:
                j = layer // 2
                if CFG["mixer"]:
                    prenorm("mpre%d" % layer, None)
                    if layer % 2 == 0:
                        ssd(j, part)
                    else:
                        lru(j, part)
                    postnorm("mpost%d" % layer)
                if CFG["ffn"]:
                    prenorm("fpre%d" % layer, None)
                    ffn(layer)
                    postnorm("fpost%d" % layer)
            S.dma("sp", lambda e, part=part: e.dma_start(out=yout[:, :, part, :], in_=X[:]), "XST", reads=XK)
        S.finish("sp")
        S.emit()
        build_nc.stats = dict(S.count), {k: len(v) for k, v in S.ops.items()}, dict(S.dcount)
    return nc


_NC_CACHE = {}


def _consts():
    c = np.zeros((128, 6, 128), np.float32)
    c[:, 0] = np.eye(128)
    t = np.arange(128)
    c[:, 1] = (t[:, None] <= t[None, :])
    c[:, 2] = (t[:, None] > t[None, :])
    c[:, 3] = 1.0
    i16 = np.eye(16, dtype=np.float32).reshape(-1)
    c[:, 4:6] = np.broadcast_to(i16.reshape(1, 2, 128), (128, 2, 128))
    return c


def kernel(x_prompt, x_sample, state_ssd_conv, state_ssd_h, state_lru_conv, state_lru_h, meta_tokens,
           norm_mix_pre, norm_mix_post, norm_ffn_pre, norm_ffn_post,
           ssd_w_in, ssd_conv_w, ssd_conv_b, ssd_dt_bias, ssd_a_log, ssd_d, ssd_norm, ssd_w_out,
           lru_w_in, lru_b_in, lru_conv_w, lru_conv_b, lru_w_a, lru_b_a, lru_w_x, lru_b_x, lru_lambda,
           lru_w_out, lru_b_out, ffn_w1, ffn_w2):
    f = lambda a: np.asarray(a, dtype=np.float32)
    x_prompt, x_sample = f(x_prompt), f(x_sample)
    NCORE = 8
    vm = VecMap()
    norms = {"mpre": f(norm_mix_pre), "mpost": f(norm_mix_post), "fpre": f(norm_ffn_pre), "fpost": f(norm_ffn_post)}
    for i in range(4):
        for kind in ("mpre", "mpost", "fpre", "fpost"):
            vm.add("%s%d" % (kind, i), fm(norms[kind][i]))
    for j in range(2):
        cw = f(ssd_conv_w)[j]
        vm.add("scw%d" % j, cw.reshape(4, 32, 128).transpose(2, 1, 0).reshape(128, 128))
        vm.add("scb%d" % j, fm(f(ssd_conv_b)[j]))
        vm.add("snorm%d" % j, fm(f(ssd_norm)[j]))
        dtb = np.zeros((128, 1), np.float32)
        dtb[0:32, 0] = f(ssd_dt_bias)[j]
        vm.add("sdt%d" % j, dtb)
        vm.add("lbin%d" % j, fm(f(lru_b_in)[j]))
        lw = f(lru_conv_w)[j]
        vm.add("lcw%d" % j, lw.reshape(4, 8, 128).transpose(2, 1, 0).reshape(128, 32))
        vm.add("lcb%d" % j, fm(f(lru_conv_b)[j]))
        vm.add("lba%d" % j, fm(f(lru_b_a)[j]))
        vm.add("lbx%d" % j, fm(f(lru_b_x)[j]))
        vm.add("llam%d" % j, fm(f(lru_lambda)[j]))
        vm.add("lbout%d" % j, fm(f(lru_b_out)[j]))
    assert vm.cols == VCOLS and vm.n == NV
    vec = vm.table()
    vecb = np.zeros((1, 128), np.float32)
    for j in range(2):
        vecb[0, j * 64:j * 64 + 32] = f(ssd_a_log)[j]
        vecb[0, j * 64 + 32:j * 64 + 64] = f(ssd_d)[j]

    def kp(w):
        K, N = w.shape
        return np.ascontiguousarray(w.reshape(K // 128, 128, N).transpose(1, 0, 2))

    DI = 2048
    w_ssd_in = np.zeros((2, 8, 2, 128, 8, 384), np.float32)
    w_ssd_dt = np.zeros((2, 128, 8, 32), np.float32)
    w_ssd_out = np.zeros((2, 4, 128, 16, 256), np.float32)
    for j in range(2):
        w = f(ssd_w_in)[j]
        for g in range(8):
            cols = np.concatenate([
                np.arange(256 * g, 256 * g + 256),
                DI + np.arange(256 * g, 256 * g + 256),
                DI + 2048 + np.arange(128 * g, 128 * g + 128),
                DI + 3072 + np.arange(128 * g, 128 * g + 128)])
            blk = kp(w[:, cols])
            w_ssd_in[j, g, 0] = blk[:, :, 0:384]
            w_ssd_in[j, g, 1] = blk[:, :, 384:768]
        w_ssd_dt[j] = kp(w[:, DI + 4096:DI + 4096 + 32])
        wo = kp(f(ssd_w_out)[j])
        for q in range(4):
            w_ssd_out[j, q] = wo[:, :, q * 256:(q + 1) * 256]
    w_lru_in = np.zeros((2, 8, 128, 8, 256), np.float32)
    w_lru_ax = np.zeros((2, 128, 8, 256), np.float32)
    w_lru_out = np.zeros((2, 2, 128, 8, 512), np.float32)
    for j in range(2):
        w = kp(f(lru_w_in)[j])
        for k in range(8):
            w_lru_in[j, k, :, :, 0:128] = w[:, :, k * 128:(k + 1) * 128]
            w_lru_in[j, k, :, :, 128:256] = w[:, :, 1024 + k * 128:1024 + (k + 1) * 128]
            w_lru_ax[j, :, k, 0:128] = f(lru_w_a)[j, k]
            w_lru_ax[j, :, k, 128:256] = f(lru_w_x)[j, k]
        wo = kp(f(lru_w_out)[j])
        for q in range(2):
            w_lru_out[j, q] = wo[:, :, q * 512:(q + 1) * 512]
    w_ffn1 = np.zeros((4, 8, 128, 8, 512), np.float32)
    w_ffn2 = np.zeros((4, 8, 128, 32, 128), np.float32)
    for i in range(4):
        w1 = kp(f(ffn_w1)[i])
        w2 = kp(f(ffn_w2)[i])
        for q in range(8):
            w_ffn1[i, q] = w1[:, :, q * 512:(q + 1) * 512]
            w_ffn2[i, q] = w2[:, :, q * 128:(q + 1) * 128]
    cst = _consts()

    in_maps = []
    meta = f(meta_tokens)
    ssc, ssh, slc, slh = f(state_ssd_conv), f(state_ssd_h), f(state_lru_conv), f(state_lru_h)
    for c in range(NCORE):
        seq = np.concatenate([meta, x_prompt[c]], axis=0)
        cols = np.zeros((NPART, TL, D), np.float32)
        cols[0] = seq[0:528]
        cols[1, 0:512] = seq[528:1040]
        cols[2, 0:512] = seq[1040:1552]
        cols[3, 0:512] = seq[1552:2064]
        cols[3, 512:528] = x_sample[c * NB:(c + 1) * NB, 0]
        xin = np.ascontiguousarray(cols.reshape(NPART, TL, 8, 128).transpose(3, 2, 0, 1))
        bs = slice(c * NB, (c + 1) * NB)
        s_sconv = np.ascontiguousarray(ssc[:, bs].reshape(2, NB, 3, 32, 128).transpose(0, 4, 3, 1, 2))
        s_sh = np.ascontiguousarray(ssh[:, bs].reshape(2, NB, 2048, 128).transpose(0, 1, 3, 2))
        s_lconv = np.ascontiguousarray(slc[:, bs].reshape(2, NB, 3, 8, 128).transpose(0, 4, 3, 1, 2))
        s_lh = np.ascontiguousarray(slh[:, bs].reshape(2, NB, 8, 128).transpose(0, 3, 2, 1))
        in_maps.append(dict(xin=xin, cst=cst, vec=vec, vecb=vecb, w_ssd_in=w_ssd_in, w_ssd_dt=w_ssd_dt, w_ssd_out=w_ssd_out,
                            w_lru_in=w_lru_in, w_lru_ax=w_lru_ax, w_lru_out=w_lru_out, w_ffn1=w_ffn1, w_ffn2=w_ffn2,
                            s_sconv=s_sconv, s_sh=s_sh, s_lconv=s_lconv, s_lh=s_lh))
    if CFG["ncore"] < NCORE:
        in_maps = in_maps[:CFG["ncore"]]
    if "nc" not in _NC_CACHE:
        _NC_CACHE["nc"] = build_nc()
    nc = _NC_CACHE["nc"]
    if CFG.get("trace"):
        res = run_bass_kernel_spmd(nc, in_maps, core_ids=list(range(len(in_maps))), trace=True)
        print("EXEC_TIME_NS", res.exec_time_ns)
    else:
        res = run_bass_kernel_spmd(nc, in_maps, core_ids=list(range(len(in_maps))))
    R = list(res.results)
    while len(R) < NCORE:
        R.append(R[0])

    y_prompt = np.zeros((8, 2048, D), np.float32)
    y_sample = np.zeros((128, 1, D), np.float32)
    p_ssd_conv = np.zeros((2, 8, 3, 4096), np.float32)
    p_ssd_h = np.zeros((2, 8, 32, 64, 128), np.float32)
    p_lru_conv = np.zeros((2, 8, 3, 1024), np.float32)
    p_lru_h = np.zeros((2, 8, 1024), np.float32)
    s_ssd_conv = np.zeros((2, 128, 3, 4096), np.float32)
    s_ssd_h = np.zeros((2, 128, 32, 64, 128), np.float32)
    s_lru_conv = np.zeros((2, 128, 3, 1024), np.float32)
    s_lru_h = np.zeros((2, 128, 1024), np.float32)
    for c in range(NCORE):
        r = R[c]
        yo = np.asarray(r["yout"]).transpose(2, 3, 1, 0).reshape(NPART, TL, D)
        seq = np.concatenate([yo[0, 0:528], yo[1, 0:512], yo[2, 0:512], yo[3, 0:512]], axis=0)
        y_prompt[c] = seq[16:]
        bs = slice(c * NB, (c + 1) * NB)
        y_sample[bs, 0] = yo[3, 512:528]
        p_ssd_conv[:, c] = np.asarray(r["o_psconv"]).transpose(0, 3, 2, 1).reshape(2, 3, 4096)
        s_ssd_conv[:, bs] = np.asarray(r["o_ssconv"]).transpose(0, 3, 4, 2, 1).reshape(2, NB, 3, 4096)
        p_ssd_h[:, c] = np.asarray(r["o_psh"]).transpose(0, 2, 1).reshape(2, 32, 64, 128)
        s_ssd_h[:, bs] = np.asarray(r["o_ssh"]).transpose(0, 1, 3, 2).reshape(2, NB, 32, 64, 128)
        p_lru_conv[:, c] = np.asarray(r["o_plconv"]).transpose(0, 3, 2, 1).reshape(2, 3, 1024)
        s_lru_conv[:, bs] = np.asarray(r["o_slconv"]).transpose(0, 3, 4, 2, 1).reshape(2, NB, 3, 1024)
        p_lru_h[:, c] = np.asarray(r["o_plh"]).reshape(2, 1024)
        s_lru_h[:, bs] = np.asarray(r["o_slh"]).transpose(0, 3, 2, 1).reshape(2, NB, 1024)
    return (y_prompt, y_sample, p_ssd_conv, p_ssd_h, p_lru_conv, p_lru_h,
            s_ssd_conv, s_ssd_h, s_lru_conv, s_lru_h)
```
